# Optimizing a Trainium2 kernel written in Bass

```python
import jax, jax.numpy as jnp
from jax import lax
import numpy as np

D_MODEL = 1024
BATCH = 4
SEQ = 8192
DEPTH = 2
DEC_BATCH = 8
DEC_SEQ = 32
PAST_LEN = 2048

CHUNK = 64
Q_BLOCK = 128
PLE_DIM = 256
D_FF = 4 * D_MODEL
MLA_HEADS = 8
MLA_NOPE = 64
MLA_ROPE = 32
MLA_V = 64
MLA_Q_LORA = 384
MLA_KV_LORA = 256
MLA_SCALE = (MLA_NOPE + MLA_ROPE) ** -0.5
RET_HEADS = 8
RET_DK = 32
RET_DV = 64
D_MIX = MLA_HEADS * MLA_V + RET_HEADS * RET_DV
ROPE_THETA = 10000.0
EPS = 1e-6
NEG = -1e30
IN_SPLITS = (MLA_Q_LORA, MLA_KV_LORA, MLA_ROPE, RET_HEADS * RET_DK, RET_HEADS * RET_DK,
             RET_HEADS * RET_DV, RET_HEADS * RET_DV)
D_IN = sum(IN_SPLITS)

kernel_name = "hybrid_mla_retention_streaming_step"


def rms_norm(x, w):
    xf = x.astype(jnp.float32)
    y = xf * lax.rsqrt(jnp.mean(xf * xf, axis=-1, keepdims=True) + EPS)
    return (y * w.astype(jnp.float32)).astype(x.dtype)


def rope(x, pos):
    half = x.shape[-1] // 2
    inv = ROPE_THETA ** (-jnp.arange(half, dtype=jnp.float32) / half)
    ang = pos.astype(jnp.float32)[:, None] * inv[None, :]
    if x.ndim == 4:
        ang = ang[:, None, :]
    cos, sin = jnp.cos(ang), jnp.sin(ang)
    xf = x.astype(jnp.float32)
    x1, x2 = xf[..., :half], xf[..., half:]
    return jnp.concatenate([x1 * cos - x2 * sin, x1 * sin + x2 * cos], axis=-1).astype(x.dtype)


def project_mixers(a, pos, w_in, q_norm_w, w_uq, kv_norm_w):
    B, L, _ = a.shape
    offs = np.cumsum(IN_SPLITS)[:-1].tolist()
    q_lat, ckv, krope, rq, rk, rv, rg = jnp.split(a @ w_in, offs, axis=-1)
    q = (rms_norm(q_lat, q_norm_w) @ w_uq).reshape(B, L, MLA_HEADS, MLA_NOPE + MLA_ROPE)
    q_nope = q[..., :MLA_NOPE]
    q_rope = rope(q[..., MLA_NOPE:], pos)
    ckv = rms_norm(ckv, kv_norm_w)
    krope = rope(krope, pos)
    rq = rope(rq.reshape(B, L, RET_HEADS, RET_DK), pos)
    rk = rope(rk.reshape(B, L, RET_HEADS, RET_DK), pos) * (RET_DK ** -0.5)
    rv = rv.reshape(B, L, RET_HEADS, RET_DV)
    return q_nope, q_rope, ckv, krope, rq, rk, rv, rg


def expand_latent(ckv, w_ukv):
    B, L, _ = ckv.shape
    kv = (ckv @ w_ukv).reshape(B, L, MLA_HEADS, MLA_NOPE + MLA_V)
    return kv[..., :MLA_NOPE], kv[..., MLA_NOPE:]


def mla_core(q_nope, q_rope, k_nope, k_rope, v, q_pos, k_pos):
    s = (jnp.einsum('bqhd,bkhd->bhqk', q_nope.astype(jnp.float32), k_nope.astype(jnp.float32))
         + jnp.einsum('bqhr,bkr->bhqk', q_rope.astype(jnp.float32), k_rope.astype(jnp.float32)))
    s = s * MLA_SCALE
    mask = (k_pos // CHUNK)[None, :] <= (q_pos // CHUNK)[:, None]
    p = jax.nn.softmax(jnp.where(mask[None, None], s, NEG), axis=-1)
    return jnp.einsum('bhqk,bkhe->bqhe', p, v.astype(jnp.float32)).astype(v.dtype)


def mla_prompt(q_nope, q_rope, k_nope, k_rope, v, pos):
    B, S = q_nope.shape[:2]
    nb = S // Q_BLOCK
    def blocks(t):
        return t.reshape(B, nb, Q_BLOCK, *t.shape[2:]).swapaxes(0, 1)
    def one_block(args):
        qn, qr, qp = args
        return mla_core(qn, qr, k_nope, k_rope, v, qp, pos)
    out = lax.map(one_block, (blocks(q_nope), blocks(q_rope), pos.reshape(nb, Q_BLOCK)))
    return out.swapaxes(0, 1).reshape(B, S, MLA_HEADS * MLA_V)


def retention_block(q, k, v, state, log_gamma):
    L = q.shape[1]
    qf, kf, vf = q.astype(jnp.float32), k.astype(jnp.float32), v.astype(jnp.float32)
    sf = state.astype(jnp.float32)
    idx = jnp.arange(L, dtype=jnp.float32)
    diff = idx[:, None] - idx[None, :]
    decay = jnp.where(diff >= 0, jnp.exp(jnp.maximum(diff, 0.0)[None] * log_gamma[:, None, None]), 0.0)
    inner = jnp.einsum('bnhd,bmhd->bhnm', qf, kf) * decay[None]
    o = jnp.einsum('bhnm,bmhe->bnhe', inner, vf)
    q_decay = jnp.exp((idx + 1.0)[:, None] * log_gamma[None, :])
    o = o + jnp.einsum('bnhd,bhde->bnhe', qf * q_decay[None, :, :, None], sf)
    k_decay = jnp.exp((L - 1.0 - idx)[:, None] * log_gamma[None, :])
    new_state = (jnp.exp(L * log_gamma)[None, :, None, None] * sf
                 + jnp.einsum('bmhd,bmhe->bhde', kf * k_decay[None, :, :, None], vf))
    return o, new_state


def retention_prompt(q, k, v, log_gamma):
    B, S = q.shape[:2]
    nc = S // CHUNK
    def chunks(t):
        return t.reshape(B, nc, CHUNK, *t.shape[2:]).swapaxes(0, 1)
    def step(st, c):
        qc, kc, vc = c
        o, st = retention_block(qc, kc, vc, st, log_gamma)
        return st, o
    st0 = jnp.zeros((B, RET_HEADS, RET_DK, RET_DV), jnp.float32)
    st, o = lax.scan(step, st0, (chunks(q), chunks(k), chunks(v)))
    return o.swapaxes(0, 1).reshape(B, S, RET_HEADS, RET_DV), st


def retention_out(o, g, gn_w):
    B, L = o.shape[:2]
    mu = jnp.mean(o, axis=-1, keepdims=True)
    var = jnp.mean(jnp.square(o - mu), axis=-1, keepdims=True)
    on = ((o - mu) * lax.rsqrt(var + EPS)).reshape(B, L, RET_HEADS * RET_DV) * gn_w.astype(jnp.float32)
    return (jax.nn.silu(g.astype(jnp.float32)) * on).astype(g.dtype)


def channel_and_ple(h, p_i, norm_ffn_w, w_ff1, w_ff2, norm_ple_w, w_ple_gate, w_ple_proj):
    u = rms_norm(h, norm_ffn_w) @ w_ff1
    h = h + jnp.square(jax.nn.relu(u)) @ w_ff2
    gate = jax.nn.sigmoid(rms_norm(h, norm_ple_w) @ w_ple_gate)
    return h + (p_i @ w_ple_proj) * gate


def setup_inputs(seed: int = 0) -> dict:
    key = jax.random.key(seed)
    ks = jax.random.split(key, 24)
    f32 = jnp.float32
    def nrm(k, shape, scale=1.0):
        return jax.random.normal(k, shape, f32) * scale
    def gain(k, shape):
        return 1.0 + 0.05 * jax.random.normal(k, shape, f32)
    return {
        'x_prompt': nrm(ks[0], (BATCH, SEQ, D_MODEL)),
        'x_sample': nrm(ks[1], (DEC_BATCH, DEC_SEQ, D_MODEL)),
        'cache_ckv': nrm(ks[2], (DEPTH, DEC_BATCH, PAST_LEN, MLA_KV_LORA)),
        'cache_krope': nrm(ks[3], (DEPTH, DEC_BATCH, PAST_LEN, MLA_ROPE)),
        'state_ret': nrm(ks[4], (DEPTH, DEC_BATCH, RET_HEADS, RET_DK, RET_DV)),
        'p_prompt': nrm(ks[5], (DEPTH, BATCH, SEQ, PLE_DIM)),
        'p_sample': nrm(ks[6], (DEPTH, DEC_BATCH, DEC_SEQ, PLE_DIM)),
        'norm_mix_w': gain(ks[7], (DEPTH, D_MODEL)),
        'w_in': nrm(ks[8], (DEPTH, D_MODEL, D_IN), D_MODEL ** -0.5),
        'q_norm_w': gain(ks[9], (DEPTH, MLA_Q_LORA)),
        'w_uq': nrm(ks[10], (DEPTH, MLA_Q_LORA, MLA_HEADS * (MLA_NOPE + MLA_ROPE)), MLA_Q_LORA ** -0.5),
        'kv_norm_w': gain(ks[11], (DEPTH, MLA_KV_LORA)),
        'w_ukv': nrm(ks[12], (DEPTH, MLA_KV_LORA, MLA_HEADS * (MLA_NOPE + MLA_V)), MLA_KV_LORA ** -0.5),
        'ret_gn_w': gain(ks[13], (DEPTH, RET_HEADS * RET_DV)),
        'w_out': nrm(ks[14], (DEPTH, D_MIX, D_MODEL), D_MIX ** -0.5),
        'norm_ffn_w': gain(ks[15], (DEPTH, D_MODEL)),
        'w_ff1': nrm(ks[16], (DEPTH, D_MODEL, D_FF), D_MODEL ** -0.5),
        'w_ff2': nrm(ks[17], (DEPTH, D_FF, D_MODEL), D_FF ** -0.5),
        'norm_ple_w': gain(ks[18], (DEPTH, D_MODEL)),
        'w_ple_gate': nrm(ks[19], (DEPTH, D_MODEL, D_MODEL), D_MODEL ** -0.5),
        'w_ple_proj': nrm(ks[20], (DEPTH, PLE_DIM, D_MODEL), PLE_DIM ** -0.5),
        'final_norm_w': gain(ks[21], (D_MODEL,)),
    }


def reference(x_prompt, x_sample, cache_ckv, cache_krope, state_ret, p_prompt, p_sample,
              norm_mix_w, w_in, q_norm_w, w_uq, kv_norm_w, w_ukv, ret_gn_w, w_out,
              norm_ffn_w, w_ff1, w_ff2, norm_ple_w, w_ple_gate, w_ple_proj, final_norm_w):
    log_gamma = jnp.log1p(-jnp.exp2(-5.0 - jnp.arange(RET_HEADS, dtype=jnp.float32)))
    B, S = x_prompt.shape[:2]
    Bd, Ld = x_sample.shape[:2]
    past = cache_ckv.shape[2]
    pos_p = jnp.arange(S, dtype=jnp.int32)
    pos_s = past + jnp.arange(Ld, dtype=jnp.int32)
    pos_all = jnp.arange(past + Ld, dtype=jnp.int32)

    hp, hs = x_prompt, x_sample
    ckv_p_l, kr_p_l, st_p_l, ckv_s_l, kr_s_l, st_s_l = [], [], [], [], [], []
    for i in range(DEPTH):
        a = rms_norm(hp, norm_mix_w[i])
        qn, qr, ckv, kr, rq, rk, rv, rg = project_mixers(a, pos_p, w_in[i], q_norm_w[i], w_uq[i], kv_norm_w[i])
        k_nope, v = expand_latent(ckv, w_ukv[i])
        att = mla_prompt(qn, qr, k_nope, kr, v, pos_p)
        ret, st = retention_prompt(rq, rk, rv, log_gamma)
        mix = jnp.concatenate([att, retention_out(ret, rg, ret_gn_w[i])], axis=-1)
        hp = hp + mix @ w_out[i]
        hp = channel_and_ple(hp, p_prompt[i], norm_ffn_w[i], w_ff1[i], w_ff2[i],
                             norm_ple_w[i], w_ple_gate[i], w_ple_proj[i])
        ckv_p_l.append(ckv)
        kr_p_l.append(kr)
        st_p_l.append(st.astype(x_prompt.dtype))

        a = rms_norm(hs, norm_mix_w[i])
        qn, qr, ckv, kr, rq, rk, rv, rg = project_mixers(a, pos_s, w_in[i], q_norm_w[i], w_uq[i], kv_norm_w[i])
        ckv_all = jnp.concatenate([cache_ckv[i], ckv], axis=1)
        kr_all = jnp.concatenate([cache_krope[i], kr], axis=1)
        k_nope, v = expand_latent(ckv_all, w_ukv[i])
        att = mla_core(qn, qr, k_nope, kr_all, v, pos_s, pos_all).reshape(Bd, Ld, MLA_HEADS * MLA_V)
        ret, st = retention_block(rq, rk, rv, state_ret[i], log_gamma)
        mix = jnp.concatenate([att, retention_out(ret, rg, ret_gn_w[i])], axis=-1)
        hs = hs + mix @ w_out[i]
        hs = channel_and_ple(hs, p_sample[i], norm_ffn_w[i], w_ff1[i], w_ff2[i],
                             norm_ple_w[i], w_ple_gate[i], w_ple_proj[i])
        ckv_s_l.append(ckv)
        kr_s_l.append(kr)
        st_s_l.append(st.astype(x_sample.dtype))

    y_prompt = rms_norm(hp, final_norm_w)
    y_sample = rms_norm(hs, final_norm_w)
    ckv_prompt = jnp.stack(ckv_p_l)
    krope_prompt = jnp.stack(kr_p_l)
    ret_prompt = jnp.stack(st_p_l)
    ckv_sample = jnp.stack(ckv_s_l)
    krope_sample = jnp.stack(kr_s_l)
    ret_sample = jnp.stack(st_s_l)
    return (y_prompt, y_sample, ckv_prompt, krope_prompt, ret_prompt, ckv_sample, krope_sample, ret_sample)
```

```python
import numpy as np
from contextlib import ExitStack
import concourse.bass as bass
import concourse.mybir as mybir
from concourse.bass_utils import run_bass_kernel_spmd

F32 = mybir.dt.float32
BF16 = mybir.dt.bfloat16
AF = mybir.ActivationFunctionType
ALU = mybir.AluOpType
AX = mybir.AxisListType

D = 1024
DFF = 4096
NH = 8
LS = 32
EPS = 1e-6
MLA_SCALE = 96 ** -0.5
NZ = 2752
import os as _os
KNT = int(_os.environ.get('KNT', '-1'))
ZOFF = [0, 448, 704, 1216, 1728, 2240, 2752]


def I(name, *a, **kw):
    return lambda e: getattr(e, name)(*a, **kw)


class Buf:
    __slots__ = ("name", "w", "r", "sem", "cnt")

    def __init__(self, name):
        self.name = name
        self.w = None
        self.r = []
        self.sem = None
        self.cnt = 0


class T:
    __slots__ = ("ap", "b")

    def __init__(self, ap, name):
        self.ap = ap
        self.b = Buf(name)


class Prog:
    ENGS = ("pe", "act", "dve", "pool", "sp")

    def __init__(self, nc, stack, tag):
        self.nc = nc
        self.stack = stack
        self.tag = tag
        self.ops = {e: [] for e in self.ENGS}
        self.sem = {}
        self.cnt = {e: 0 for e in self.ENGS}
        self.waited = {e: {} for e in self.ENGS}
        for e in ("pe", "act", "dve", "pool"):
            self.sem[e] = nc.alloc_semaphore(name=f"s{tag}_{e}")
        self.dma_bufs = []
        self.ninstr = 0

    def _wait(self, eng, tok):
        if tok is None:
            return
        sem, val, src = tok
        key = id(sem)
        if self.waited[eng].get(key, 0) >= val:
            return
        self.waited[eng][key] = val
        self.ops[eng].append(lambda e, sem=sem, val=val: e.wait_ge(sem, val))

    def _deps(self, eng, reads, writes):
        for b in reads:
            self._wait(eng, b.w)
        for b in writes:
            if b.w is not None and b.w[2] != eng:
                self._wait(eng, b.w)
            for t in b.r:
                if t[2] != eng:
                    self._wait(eng, t)

    def _mark(self, tok, reads, writes):
        for b in reads:
            b.r.append(tok)
            if len(b.r) > 12:
                last = {}
                for t in b.r:
                    k = id(t[0])
                    if k not in last or last[k][1] < t[1]:
                        last[k] = t
                b.r = list(last.values())
        for b in writes:
            b.w = tok
            b.r = []

    def op(self, eng, fns, reads=(), writes=()):
        if callable(fns):
            fns = [fns]
        reads = [x.b if isinstance(x, T) else x for x in reads]
        writes = [x.b if isinstance(x, T) else x for x in writes]
        self._deps(eng, reads, writes)
        self.cnt[eng] += 1
        sem = self.sem[eng]
        tok = (sem, self.cnt[eng], eng)
        for f in fns[:-1]:
            self.ops[eng].append(f)
        last = fns[-1]
        self.ops[eng].append(lambda e, last=last, sem=sem: last(e).then_inc(sem, 1))
        self.ninstr += len(fns)
        self._mark(tok, reads, writes)
        return tok

    def dma(self, q, out, in_, reads=(), writes=()):
        reads = [x.b if isinstance(x, T) else x for x in reads]
        writes = [x.b if isinstance(x, T) else x for x in writes]
        owner = writes[0] if writes else reads[0]
        if owner.sem is None:
            owner.sem = self.nc.alloc_semaphore(name=f"d{self.tag}_{len(self.dma_bufs)}")
            self.dma_bufs.append(owner)
        self._deps(q, reads, writes)
        owner.cnt += 16
        sem = owner.sem
        tok = (sem, owner.cnt, None)
        self.ops[q].append(lambda e, out=out, in_=in_, sem=sem: e.dma_start(out=out, in_=in_).then_inc(sem, 16))
        self.ninstr += 1
        self._mark(tok, reads, writes)
        return tok

    def finish(self):
        for e in self.ENGS:
            for s in ("pe", "act", "dve", "pool"):
                if s != e and self.cnt[s]:
                    self._wait(e, (self.sem[s], self.cnt[s], s))
            for b in self.dma_bufs:
                self._wait(e, (b.sem, b.cnt, None))
        allsems = [self.sem[s] for s in ("pe", "act", "dve", "pool")] + [b.sem for b in self.dma_bufs]
        for b in self.dma_bufs:
            b.sem = None
            b.cnt = 0
            b.w = None
            b.r = []
        ops = self.ops
        with self.nc.Block() as block:
            @block.tensor
            def _(e):
                for f in ops["pe"]:
                    f(e)

            @block.scalar
            def _(e):
                for f in ops["act"]:
                    f(e)

            @block.vector
            def _(e):
                for f in ops["dve"]:
                    f(e)

            @block.gpsimd
            def _(e):
                for f in ops["pool"]:
                    f(e)

            @block.sync
            def _(e):
                for f in ops["sp"]:
                    f(e)
        self.nc.all_engine_barrier()
        self.nc.clear_and_free_semaphores(allsems)
        self.nc.all_engine_barrier()


class Ring:
    def __init__(self, items):
        self.items = items
        self.i = 0

    def next(self):
        x = self.items[self.i % len(self.items)]
        self.i += 1
        return x


def bc_mid(ap, n):
    p, f = ap.shape
    return ap.unsqueeze(1).to_broadcast([p, n, f])


def bc_last(ap, n):
    p, k = ap.shape
    return ap.unsqueeze(2).to_broadcast([p, k, n])


class Builder:
    def __init__(self, S, PAST):
        self.S, self.PAST = S, PAST
        self.NK_S = PAST + LS
        nc = bass.Bass("TRN2", target_bir_lowering=False)
        self.nc = nc
        di = lambda n, s, dt=F32: nc.dram_tensor(n, s, dt, kind="ExternalInput").ap()
        do = lambda n, s: nc.dram_tensor(n, s, F32, kind="ExternalOutput").ap()
        import os
        dbg = os.environ.get("KDBG") == "1"
        ds = lambda n, s, dt: nc.dram_tensor(n, s, dt, kind=("ExternalOutput" if dbg else "Internal")).ap()
        self.x = di("x", [S, D]); self.xs = di("xs", [LS, D])
        self.cckv = di("cckv", [2, PAST, 256]); self.ckr = di("ckr", [2, PAST, 32])
        self.sret = di("sret", [2, NH, 32, 64])
        self.pp = di("pp", [2, S, 256]); self.pps = di("pps", [2, LS, 256])
        self.w_in = di("w_in", [2, D, NZ]); self.w_uq = di("w_uq", [2, 384, 1024])
        self.w_k = di("w_k", [2, 256, 512]); self.w_v = di("w_v", [2, 256, 512])
        self.w_out = di("w_out", [2, D, D]); self.w_ff1 = di("w_ff1", [2, D, DFF])
        self.w_ff2 = di("w_ff2", [2, DFF, D]); self.w_gate = di("w_gate", [2, D, D])
        self.w_proj = di("w_proj", [2, 256, D])
        self.nw1 = di("nw1", [2, D]); self.qnw = di("qnw", [2, 384]); self.kvnw = di("kvnw", [2, 256])
        self.gnw = di("gnw", [2, 512]); self.nw2 = di("nw2", [2, D]); self.nw3 = di("nw3", [2, D])
        self.fnw = di("fnw", [1, D])
        self.tab_p = di("tab_p", [S, 128]); self.tab_s = di("tab_s", [LS, 128])
        self.cst = di("cst", [128, 2176])
        self.y = do("y", [S, D]); self.ys = do("ys", [LS, D])
        self.ckv_o = do("ckv_o", [2, S, 256]); self.kr_o = do("kr_o", [2, S, 32])
        self.ret_o = do("ret_o", [2, NH, 32, 64])
        self.ckvs_o = do("ckvs_o", [2, LS, 256]); self.krs_o = do("krs_o", [2, LS, 32])
        self.rets_o = do("rets_o", [2, NH, 32, 64])
        self.qT_scr = ds("qT_scr", [NH, 96, S], BF16); self.qTs_scr = ds("qTs_scr", [NH, 96, LS], BF16)
        self.mixT_scr = ds("mixT_scr", [D, S], BF16); self.mixTs_scr = ds("mixTs_scr", [D, LS], BF16)
        self.hbuf = ds("hbuf", [S, D], F32); self.hsbuf = ds("hsbuf", [LS, D], F32)
        self.ckvT_scr = ds("ckvT_scr", [256, S], BF16); self.ckvTs_scr = ds("ckvTs_scr", [256, LS], BF16)
        self.krT_scr = ds("krT_scr", [32, S], BF16); self.krTs_scr = ds("krTs_scr", [32, LS], BF16)
        self.uid = 0

    def sb(self, st, shape, dt, name=None):
        self.uid += 1
        nm = f"{name or 't'}_{self.uid}"
        t = st.enter_context(self.nc.sbuf_tensor(nm, list(shape), dt))
        return T(t[:], nm)

    def psum(self, st):
        banks = []
        for i in range(8):
            self.uid += 1
            t = st.enter_context(self.nc.psum_tensor(f"ps{self.uid}", [128, 512], F32))
            banks.append(T(t[:], f"ps{i}"))
        return banks

    def load_w(self, P, dst, src_kpn):
        K = dst.ap.shape[1]
        N = dst.ap.shape[2]
        step = max(1, 8192 // N)
        for k0 in range(0, K, step):
            k1 = min(K, k0 + step)
            P.dma("pool", dst.ap[:, k0:k1, :],
                  src_kpn[k0 * 128:k1 * 128, :].rearrange("(k p) n -> p k n", p=128), writes=[dst])

    def load_bc(self, P, dst, src_row):
        n = src_row.shape[1]
        P.dma("sp", dst.ap, src_row.to_broadcast([128, n]), writes=[dst])

    def rstd_from_ss(self, P, ss, rstd, n, cols, inv_d):
        P.op("act", I("activation", out=rstd.ap[:n, :cols], in_=ss.ap[:n, :cols], func=AF.Ln,
                                           scale=inv_d, bias=EPS), reads=[ss], writes=[rstd])
        P.op("act", I("activation", out=rstd.ap[:n, :cols], in_=rstd.ap[:n, :cols], func=AF.Exp,
                                           scale=-0.5), reads=[rstd], writes=[rstd])

    def transpose_to(self, P, ident, src, src_aps, pt, dst, dst_ap_fn, npart_out, n, copy_eng="dve"):
        m = len(src_aps)
        ptv = pt.ap.bitcast(BF16).rearrange("p (k c) -> p k c", c=128)
        P.op("pe", [I("transpose", out=ptv[:a.shape[1], i, :n], in_=a, identity=ident.ap[:n, :n])
                    for i, a in enumerate(src_aps)], reads=[src, ident], writes=[pt])
        dap = dst_ap_fn()
        if copy_eng == "act":
            P.op("act", I("activation", out=dap, in_=ptv[:npart_out, :m, :n], func=AF.Copy),
                 reads=[pt], writes=[dst])
        else:
            P.op(copy_eng, I("tensor_copy", out=dap, in_=ptv[:npart_out, :m, :n]), reads=[pt], writes=[dst])

    def phase_A(self, l, hsrc_p, hsrc_s, L):
        nc = self.nc
        S, PAST = self.S, self.PAST
        with ExitStack() as st:
            P = Prog(nc, st, f"A{l}")
            sb = lambda shape, dt, name=None: self.sb(st, shape, dt, name)
            PS = self.psum(st)
            w_in = sb([128, 8, NZ], BF16, "w_in"); w_uq = sb([128, 3, 1024], BF16, "w_uq")
            self.load_w(P, w_in, self.w_in[l]); self.load_w(P, w_uq, self.w_uq[l])
            nw1 = sb([128, D], F32); qnw = sb([128, 384], F32); kvnw = sb([128, 256], F32); gnw = sb([128, 512], F32)
            self.load_bc(P, nw1, self.nw1[l:l + 1, :]); self.load_bc(P, qnw, self.qnw[l:l + 1, :])
            self.load_bc(P, kvnw, self.kvnw[l:l + 1, :]); self.load_bc(P, gnw, self.gnw[l:l + 1, :])
            cstf = sb([128, 2176], F32, "cstf")
            P.dma("sp", cstf.ap, self.cst, writes=[cstf])
            cstb = sb([128, 768], BF16, "cstb")
            P.op("dve", I("tensor_copy", out=cstb.ap, in_=cstf.ap[:, 0:768]), reads=[cstf], writes=[cstb])
            ident = T(cstb.ap[:, 0:128], "ident"); ident.b = cstb.b
            causal = cstb.ap[:, 128:256]
            bmq = cstb.ap[:, 256:768].rearrange("p (j n) -> p j n", j=4)
            bms = cstf.ap[:, 768:1024]
            Dq = cstf.ap[:, 1024:1280]; Dk = cstf.ap[:, 1280:1536]
            gl2 = {128: cstf.ap[:, 2048:2050], 32: cstf.ap[:, 2050:2052]}

            hbR = Ring([sb([128, D], F32, "hb") for _ in range(2)])
            tbR = Ring([sb([128, 128], F32, "tb") for _ in range(3)])
            zR = Ring([sb([128, NZ], F32, "z") for _ in range(3)])
            osbR = Ring([sb([128, 512], F32, "o_sb") for _ in range(2)])
            junk = sb([128, D], BF16, "junk")
            st1 = sb([128, 4], F32, "st1"); stq = sb([128, 4], F32, "stq"); stk = sb([128, 4], F32, "stk")
            a_bf = sb([128, D], BF16, "a_bf"); aT = sb([128, 8, 128], BF16, "aT")
            qn_bf = sb([128, 384], BF16); qnT = sb([128, 3, 128], BF16)
            t1q = sb([128, 256], F32, "t1q"); t2q = sb([128, 256], F32, "t2q")
            t1k = sb([128, 32], F32, "t1k"); t2k = sb([128, 32], F32, "t2k")
            t1r = sb([128, 256], F32, "t1r"); t2r = sb([128, 256], F32, "t2r")
            q_bf = sb([128, NH, 96], BF16, "q_bf")
            ckvn = Ring([sb([128, 256], F32, "ckvn") for _ in range(2)])
            ckvn_bf = sb([128, 256], BF16)
            krf = Ring([sb([128, 32], F32, "krf") for _ in range(2)])
            kr_bf = sb([128, 32], BF16)
            qd_bf = sb([128, 256], BF16); kd_bf = sb([128, 256], BF16); v_bf = sb([128, 512], BF16)
            qkT = sb([128, 4, 128], BF16, "qkT"); qbd = sb([128, 2, 4, 128], BF16, "qbd")
            innerT = sb([128, NH, 128], BF16, "innerT")
            osq = sb([128, 512], F32); on = sb([128, 512], F32); eg = sb([128, 512], F32)
            gst = sb([128, 64], F32, "gst")
            mixr = sb([128, 512], BF16)
            stmp = sb([128, 512], F32, "stmp")
            qTst = [sb([96, NH, 512], BF16, "qTst") for _ in range(2)]
            mixTst = [sb([128, 4, 512], BF16, "mixTst") for _ in range(2)]
            ckvTst = [sb([128, 2, 512], BF16, "ckvTst") for _ in range(2)]
            krTst = [sb([32, 512], BF16, "krTst") for _ in range(2)]
            ptR = Ring([PS[6], PS[7]])

            states = {}
            for is_s in (False, True):
                Sst = sb([128, 2, 256], F32, "Sst"); Sbf = sb([128, 2, 256], BF16, "Sbf")
                P.op("dve", I("memset", Sst.ap, 0.0), writes=[Sst])
                if is_s:
                    for h in range(NH):
                        g, j = h // 4, h % 4
                        P.dma("sp", Sst.ap[32 * j:32 * j + 32, g, j * 64:(j + 1) * 64], self.sret[l, h], writes=[Sst])
                P.op("dve", I("tensor_copy", out=Sbf.ap, in_=Sst.ap), reads=[Sst], writes=[Sbf])
                states[is_s] = (Sst, Sbf)

            class Desc:
                pass
            descs = []
            ntp = S // 128
            for t in range(ntp + 1):
                d = Desc()
                d.is_s = (t == ntp)
                d.nt = LS if d.is_s else 128
                d.t0 = 0 if d.is_s else t * 128
                d.tt = 0 if d.is_s else t % 4
                d.sel = (t // 4) % 2
                d.flush = d.is_s or d.tt == 3 or t == ntp - 1
                d.c0 = 0 if d.is_s else (t // 4) * 512
                d.last = d.is_s or t == ntp - 1
                d.hsrc = hsrc_s if d.is_s else hsrc_p
                d.tab = self.tab_s if d.is_s else self.tab_p
                d.ckv_o = self.ckvs_o if d.is_s else self.ckv_o
                d.kr_o = self.krs_o if d.is_s else self.kr_o
                d.ret_o = self.rets_o if d.is_s else self.ret_o
                d.ckvT_scr = self.ckvTs_scr if d.is_s else self.ckvT_scr
                d.krT_scr = self.krTs_scr if d.is_s else self.krT_scr
                d.qT_scr = self.qTs_scr if d.is_s else self.qT_scr
                d.mixT_scr = self.mixTs_scr if d.is_s else self.mixT_scr
                d.Lblk = LS if d.is_s else 128
                d.Sst, d.Sbf = states[d.is_s]
                descs.append(d)

            def stage1(d):
                nt, t0 = d.nt, d.t0
                hb = hbR.next(); tb = tbR.next(); z = zR.next()
                d.tb, d.z = tb, z
                P.dma("sp", hb.ap[:nt, :], d.hsrc[t0:t0 + nt, :], writes=[hb])
                P.dma("sp", tb.ap[:nt, :], d.tab[t0:t0 + nt, :], writes=[tb])
                yield
                P.op("act", I("activation", out=junk.ap[:nt, :], in_=hb.ap[:nt, :], func=AF.Square,
                              accum_out=st1.ap[:nt, 0:1]), reads=[hb], writes=[junk, st1])
                self.rstd_from_ss(P, st1, st1, nt, 1, 1.0 / D)
                yield
                P.op("dve", I("scalar_tensor_tensor", out=a_bf.ap[:nt, :], in0=hb.ap[:nt, :], scalar=st1.ap[:nt, 0:1],
                              in1=nw1.ap[:nt, :], op0=ALU.mult, op1=ALU.mult), reads=[hb, st1, nw1], writes=[a_bf])
                yield
                self.transpose_to(P, ident, a_bf, [a_bf.ap[:nt, k * 128:(k + 1) * 128] for k in range(8)],
                                  ptR.next(), aT, lambda: aT.ap[:, :, :nt], 128, nt, "dve")
                yield
                for ps_ in range(2):
                    fns = []
                    for k in range(8):
                        for n in range(3):
                            zi = ps_ * 3 + n
                            w = ZOFF[zi + 1] - ZOFF[zi]
                            fns.append(I("matmul", PS[n].ap[:nt, :w], lhsT=aT.ap[:, k, :nt],
                                         rhs=w_in.ap[:, k, ZOFF[zi]:ZOFF[zi] + w], start=(k == 0), stop=(k == 7)))
                    P.op("pe", fns, reads=[aT, w_in], writes=PS[0:3])
                    yield
                    for n in range(3):
                        zi = ps_ * 3 + n
                        w = ZOFF[zi + 1] - ZOFF[zi]
                        if n % 2 == 0:
                            P.op("act", I("activation", out=z.ap[:nt, ZOFF[zi]:ZOFF[zi] + w], in_=PS[n].ap[:nt, :w],
                                          func=AF.Copy), reads=[PS[n]], writes=[z])
                        else:
                            P.op("dve", I("tensor_copy", out=z.ap[:nt, ZOFF[zi]:ZOFF[zi] + w], in_=PS[n].ap[:nt, :w]),
                                 reads=[PS[n]], writes=[z])
                        yield

            def chain_q(d):
                nt, t0, tt, z, tb = d.nt, d.t0, d.tt, d.z, d.tb
                P.op("act", I("activation", out=junk.ap[:nt, 0:384], in_=z.ap[:nt, 0:384], func=AF.Square,
                              accum_out=stq.ap[:nt, 0:1]), reads=[z], writes=[junk, stq])
                self.rstd_from_ss(P, stq, stq, nt, 1, 1.0 / 384)
                yield
                P.op("dve", I("scalar_tensor_tensor", out=qn_bf.ap[:nt, :], in0=z.ap[:nt, 0:384], scalar=stq.ap[:nt, 0:1],
                              in1=qnw.ap[:nt, :], op0=ALU.mult, op1=ALU.mult), reads=[z, stq, qnw], writes=[qn_bf])
                yield
                self.transpose_to(P, ident, qn_bf, [qn_bf.ap[:nt, k * 128:(k + 1) * 128] for k in range(3)],
                                  ptR.next(), qnT, lambda: qnT.ap[:, :, :nt], 128, nt, "act")
                yield
                fns = []
                for k in range(3):
                    for c in range(2):
                        fns.append(I("matmul", PS[3 + c].ap[:nt, :], lhsT=qnT.ap[:, k, :nt],
                                     rhs=w_uq.ap[:, k, c * 512:(c + 1) * 512], start=(k == 0), stop=(k == 2)))
                P.op("pe", fns, reads=[qnT, w_uq], writes=PS[3:5])
                yield
                for c in range(2):
                    qv = PS[3 + c].ap[:nt, :].rearrange("p (j d) -> p j d", d=128)
                    t1v = t1q.ap[:nt, c * 128:(c + 1) * 128].rearrange("p (j d) -> p j d", d=32)
                    t2v = t2q.ap[:nt, c * 128:(c + 1) * 128].rearrange("p (j d) -> p j d", d=32)
                    P.op("dve", I("tensor_tensor", out=t1v, in0=qv[:, :, 0:32], in1=bc_mid(tb.ap[:nt, 64:96], 4), op=ALU.mult),
                         reads=[PS[3 + c], tb], writes=[t1q])
                    P.op("dve", I("tensor_tensor", out=t2v, in0=qv[:, :, 32:64], in1=bc_mid(tb.ap[:nt, 96:128], 4), op=ALU.mult),
                         reads=[PS[3 + c], tb], writes=[t2q])
                    yield
                    P.op("dve", I("tensor_tensor", out=q_bf.ap[:nt, c * 4:(c + 1) * 4, 64:96], in0=t1v, in1=t2v, op=ALU.add),
                         reads=[t1q, t2q], writes=[q_bf])
                    P.op("act", I("activation", out=q_bf.ap[:nt, c * 4:(c + 1) * 4, 0:64], in_=qv[:, :, 64:128], func=AF.Copy,
                                  scale=MLA_SCALE), reads=[PS[3 + c]], writes=[q_bf])
                    yield
                qst = qTst[d.sel]
                self.transpose_to(P, ident, q_bf, [q_bf.ap[:nt, h, :] for h in range(NH)], ptR.next(), qst,
                                  lambda: qst.ap[:, :, tt * 128:tt * 128 + nt], 96, nt, "dve")
                yield
                if d.flush:
                    wcols = tt * 128 + nt
                    P.dma("sp", d.qT_scr[:, :, d.c0:d.c0 + wcols].rearrange("h r c -> r h c"), qst.ap[:, :, :wcols], reads=[qst])
                    yield

            def chain_kv(d):
                nt, t0, tt, z, tb = d.nt, d.t0, d.tt, d.z, d.tb
                P.op("act", I("activation", out=junk.ap[:nt, 0:256], in_=z.ap[:nt, 448:704], func=AF.Square,
                              accum_out=stk.ap[:nt, 0:1]), reads=[z], writes=[junk, stk])
                self.rstd_from_ss(P, stk, stk, nt, 1, 1.0 / 256)
                yield
                ck = ckvn.next()
                P.op("dve", I("scalar_tensor_tensor", out=ck.ap[:nt, :], in0=z.ap[:nt, 448:704], scalar=stk.ap[:nt, 0:1],
                              in1=kvnw.ap[:nt, :], op0=ALU.mult, op1=ALU.mult), reads=[z, stk, kvnw], writes=[ck])
                yield
                P.dma("sp", d.ckv_o[l, t0:t0 + nt, :], ck.ap[:nt, :], reads=[ck])
                P.op("act", I("activation", out=ckvn_bf.ap[:nt, :], in_=ck.ap[:nt, :], func=AF.Copy), reads=[ck], writes=[ckvn_bf])
                yield
                cst_ = ckvTst[d.sel]
                self.transpose_to(P, ident, ckvn_bf, [ckvn_bf.ap[:nt, k * 128:(k + 1) * 128] for k in range(2)],
                                  ptR.next(), cst_, lambda: cst_.ap[:, :, tt * 128:tt * 128 + nt], 128, nt, "act")
                yield
                kr = krf.next()
                P.op("dve", I("tensor_tensor", out=t1k.ap[:nt, :], in0=z.ap[:nt, 384:416], in1=tb.ap[:nt, 0:32], op=ALU.mult),
                     reads=[z, tb], writes=[t1k])
                P.op("dve", I("tensor_tensor", out=t2k.ap[:nt, :], in0=z.ap[:nt, 416:448], in1=tb.ap[:nt, 32:64], op=ALU.mult),
                     reads=[z, tb], writes=[t2k])
                yield
                P.op("dve", I("tensor_tensor", out=kr.ap[:nt, :], in0=t1k.ap[:nt, :], in1=t2k.ap[:nt, :], op=ALU.add),
                     reads=[t1k, t2k], writes=[kr])
                yield
                P.dma("sp", d.kr_o[l, t0:t0 + nt, :], kr.ap[:nt, :], reads=[kr])
                P.op("act", I("activation", out=kr_bf.ap[:nt, :], in_=kr.ap[:nt, :], func=AF.Copy), reads=[kr], writes=[kr_bf])
                yield
                pt = ptR.next()
                ptv = pt.ap.bitcast(BF16).rearrange("p (k c) -> p k c", c=128)
                P.op("pe", I("transpose", out=ptv[:32, 0, :nt], in_=kr_bf.ap[:nt, :], identity=ident.ap[:nt, :nt]),
                     reads=[kr_bf, ident], writes=[pt])
                kst = krTst[d.sel]
                P.op("act", I("activation", out=kst.ap[0:32, tt * 128:tt * 128 + nt], in_=ptv[:32, 0, :nt], func=AF.Copy),
                     reads=[pt], writes=[kst])
                yield
                if d.flush:
                    wcols = tt * 128 + nt
                    P.dma("sp", d.ckvT_scr[:, d.c0:d.c0 + wcols].rearrange("(k p) c -> p k c", p=128),
                          cst_.ap[:, :, :wcols], reads=[cst_])
                    P.dma("sp", d.krT_scr[:, d.c0:d.c0 + wcols], kst.ap[:, :wcols], reads=[kst])
                    yield

            def chain_ret(d):
                nt, t0, tt, z, tb = d.nt, d.t0, d.tt, d.z, d.tb
                Sst, Sbf = d.Sst, d.Sbf
                for (zo, Dtab, dst) in ((704, Dq, qd_bf), (1216, Dk, kd_bf)):
                    zv = z.ap[:nt, zo:zo + 512].rearrange("p (h d) -> p h d", d=64)
                    t1v = t1r.ap[:nt, :].rearrange("p (h d) -> p h d", d=32)
                    t2v = t2r.ap[:nt, :].rearrange("p (h d) -> p h d", d=32)
                    P.op("dve", I("tensor_tensor", out=t1v, in0=zv[:, :, 0:32], in1=bc_mid(tb.ap[:nt, 0:32], NH), op=ALU.mult),
                         reads=[z, tb], writes=[t1r])
                    yield
                    P.op("dve", I("tensor_tensor", out=t2v, in0=zv[:, :, 32:64], in1=bc_mid(tb.ap[:nt, 32:64], NH), op=ALU.mult),
                         reads=[z, tb], writes=[t2r])
                    yield
                    P.op("dve", I("tensor_tensor", out=t1r.ap[:nt, :], in0=t1r.ap[:nt, :], in1=t2r.ap[:nt, :], op=ALU.add),
                         reads=[t1r, t2r], writes=[t1r])
                    yield
                    P.op("dve", I("tensor_tensor", out=dst.ap[:nt, :], in0=t1r.ap[:nt, :], in1=Dtab[:nt, :], op=ALU.mult),
                         reads=[t1r, cstf], writes=[dst])
                    yield
                P.op("act", I("activation", out=v_bf.ap[:nt, :], in_=z.ap[:nt, 1728:2240], func=AF.Copy), reads=[z], writes=[v_bf])
                pt = ptR.next()
                ptv = pt.ap.bitcast(BF16).rearrange("p (k c) -> p k c", c=128)
                fns = []
                for g in range(2):
                    fns.append(I("transpose", out=ptv[:, g, :nt], in_=qd_bf.ap[:nt, g * 128:(g + 1) * 128], identity=ident.ap[:nt, :nt]))
                    fns.append(I("transpose", out=ptv[:, 2 + g, :nt], in_=kd_bf.ap[:nt, g * 128:(g + 1) * 128], identity=ident.ap[:nt, :nt]))
                P.op("pe", fns, reads=[qd_bf, kd_bf, ident], writes=[pt])
                yield
                P.op("act", I("activation", out=qkT.ap[:, :, :nt], in_=ptv[:, 0:4, :nt], func=AF.Copy), reads=[pt], writes=[qkT])
                yield
                for g in range(2):
                    P.op("dve", I("tensor_tensor", out=qbd.ap[:, g, :, :nt], in0=bc_mid(qkT.ap[:, g, :nt], 4), in1=bmq[:, :, :nt],
                                  op=ALU.mult), reads=[qkT, cstb], writes=[qbd])
                    yield
                fns = []
                for g in range(2):
                    ov = PS[3 + g].ap[:nt, :].rearrange("p (j n) -> p j n", j=4)[:, :, :nt]
                    fns.append(I("matmul", ov, lhsT=qkT.ap[:, 2 + g, :nt], rhs=qbd.ap[:, g, :, :nt], start=True, stop=True))
                P.op("pe", fns, reads=[qkT, qbd], writes=PS[3:5])
                yield
                for g in range(2):
                    ov = PS[3 + g].ap[:nt, :].rearrange("p (j n) -> p j n", j=4)[:, :, :nt]
                    P.op("dve", I("tensor_tensor", out=innerT.ap[:nt, g * 4:(g + 1) * 4, :nt], in0=ov, in1=bc_mid(causal[:nt, :nt], 4),
                                  op=ALU.mult), reads=[PS[3 + g], cstb], writes=[innerT])
                    yield
                fns = []
                for h in range(NH):
                    g, j = h // 4, h % 4
                    fns.append(I("matmul", PS[5].ap[:nt, h * 64:(h + 1) * 64], lhsT=qkT.ap[:, g, :nt],
                                 rhs=Sbf.ap[:, g, j * 64:(j + 1) * 64], start=True, stop=False))
                    fns.append(I("matmul", PS[5].ap[:nt, h * 64:(h + 1) * 64], lhsT=innerT.ap[:nt, h, :nt],
                                 rhs=v_bf.ap[:nt, h * 64:(h + 1) * 64], start=False, stop=True))
                P.op("pe", fns, reads=[qkT, Sbf, innerT, v_bf], writes=[PS[5]])
                yield
                fns = [I("matmul", PS[3].ap[:, g * 256:(g + 1) * 256], lhsT=kd_bf.ap[:nt, g * 128:(g + 1) * 128],
                         rhs=v_bf.ap[:nt, g * 256:(g + 1) * 256], start=True, stop=True) for g in range(2)]
                P.op("pe", fns, reads=[kd_bf, v_bf], writes=[PS[3]])
                yield
                P.op("dve", I("tensor_tensor", out=stmp.ap.rearrange("p (g c) -> p g c", g=2),
                              in0=PS[3].ap.rearrange("p (g c) -> p g c", g=2), in1=bc_mid(bms, 2), op=ALU.mult),
                     reads=[PS[3], cstf], writes=[stmp])
                yield
                P.op("dve", I("tensor_tensor", out=Sst.ap.rearrange("p g c -> p (g c)"), in0=Sst.ap.rearrange("p g c -> p (g c)"),
                              in1=stmp.ap, op=ALU.add), reads=[Sst, stmp], writes=[Sst])
                yield
                for g in range(2):
                    P.op("dve", I("tensor_scalar", out=Sst.ap[:, g, :], in0=Sst.ap[:, g, :], scalar1=gl2[d.Lblk][:, g:g + 1],
                                  scalar2=None, op0=ALU.mult), reads=[Sst, cstf], writes=[Sst])
                yield
                P.op("dve", I("tensor_copy", out=Sbf.ap, in_=Sst.ap), reads=[Sst], writes=[Sbf])
                yield
                osb = osbR.next()
                d.osb = osb
                P.op("act", I("activation", out=osb.ap[:nt, :], in_=PS[5].ap[:nt, :], func=AF.Copy), reads=[PS[5]], writes=[osb])
                yield
                if d.last:
                    for h in range(NH):
                        g, j = h // 4, h % 4
                        P.dma("sp", d.ret_o[l, h], Sst.ap[32 * j:32 * j + 32, g, j * 64:(j + 1) * 64], reads=[Sst])
                    yield

            def chain_ret_b(d):
                nt, t0, tt, z, osb = d.nt, d.t0, d.tt, d.z, d.osb
                ov = osb.ap[:nt, :].rearrange("p (h d) -> p h d", d=64)
                P.op("dve", I("tensor_reduce", out=gst.ap[:nt, 0:8], in_=ov, axis=AX.X, op=ALU.add), reads=[osb], writes=[gst])
                P.op("act", I("activation", out=osq.ap[:nt, :], in_=osb.ap[:nt, :], func=AF.Square), reads=[osb], writes=[osq])
                yield
                P.op("dve", I("tensor_reduce", out=gst.ap[:nt, 8:16], in_=osq.ap[:nt, :].rearrange("p (h d) -> p h d", d=64),
                              axis=AX.X, op=ALU.add), reads=[osq], writes=[gst])
                yield
                P.op("dve", I("tensor_scalar", out=gst.ap[:nt, 16:24], in0=gst.ap[:nt, 0:8], scalar1=1.0 / 64, scalar2=None,
                              op0=ALU.mult), reads=[gst], writes=[gst])
                yield
                P.op("dve", I("tensor_tensor", out=gst.ap[:nt, 24:32], in0=gst.ap[:nt, 16:24], in1=gst.ap[:nt, 16:24], op=ALU.mult),
                     reads=[gst], writes=[gst])
                yield
                P.op("dve", I("scalar_tensor_tensor", out=gst.ap[:nt, 32:40], in0=gst.ap[:nt, 8:16], scalar=1.0 / 64,
                              in1=gst.ap[:nt, 24:32], op0=ALU.mult, op1=ALU.subtract), reads=[gst], writes=[gst])
                yield
                P.op("act", I("activation", out=gst.ap[:nt, 40:48], in_=gst.ap[:nt, 32:40], func=AF.Ln, scale=1.0, bias=EPS),
                     reads=[gst], writes=[gst])
                P.op("act", I("activation", out=gst.ap[:nt, 40:48], in_=gst.ap[:nt, 40:48], func=AF.Exp, scale=-0.5),
                     reads=[gst], writes=[gst])
                P.op("act", I("activation", out=eg.ap[:nt, :], in_=z.ap[:nt, 2240:2752], func=AF.Exp, scale=-1.0),
                     reads=[z], writes=[eg])
                P.op("act", I("activation", out=eg.ap[:nt, :], in_=eg.ap[:nt, :], func=AF.Ln, scale=1.0, bias=1.0),
                     reads=[eg], writes=[eg])
                P.op("act", I("activation", out=eg.ap[:nt, :], in_=eg.ap[:nt, :], func=AF.Exp, scale=-1.0),
                     reads=[eg], writes=[eg])
                yield
                onv = on.ap[:nt, :].rearrange("p (h d) -> p h d", d=64)
                P.op("dve", I("tensor_tensor", out=onv, in0=ov, in1=bc_last(gst.ap[:nt, 16:24], 64), op=ALU.subtract),
                     reads=[osb, gst], writes=[on])
                yield
                P.op("dve", I("tensor_tensor", out=onv, in0=onv, in1=bc_last(gst.ap[:nt, 40:48], 64), op=ALU.mult),
                     reads=[on, gst], writes=[on])
                yield
                P.op("dve", I("tensor_tensor", out=on.ap[:nt, :], in0=on.ap[:nt, :], in1=gnw.ap[:nt, :], op=ALU.mult),
                     reads=[on, gnw], writes=[on])
                yield
                P.op("dve", I("tensor_tensor", out=eg.ap[:nt, :], in0=eg.ap[:nt, :], in1=z.ap[:nt, 2240:2752], op=ALU.mult),
                     reads=[eg, z], writes=[eg])
                yield
                P.op("dve", I("tensor_tensor", out=mixr.ap[:nt, :], in0=on.ap[:nt, :], in1=eg.ap[:nt, :], op=ALU.mult),
                     reads=[on, eg], writes=[mixr])
                yield
                mst = mixTst[d.sel]
                self.transpose_to(P, ident, mixr, [mixr.ap[:nt, k * 128:(k + 1) * 128] for k in range(4)],
                                  ptR.next(), mst, lambda: mst.ap[:, :, tt * 128:tt * 128 + nt], 128, nt, "act")
                yield
                if d.flush:
                    wcols = tt * 128 + nt
                    P.dma("sp", d.mixT_scr[512:1024, d.c0:d.c0 + wcols].rearrange("(k p) c -> p k c", p=128),
                          mst.ap[:, :, :wcols], reads=[mst])
                yield

            def interleave(gens):
                gens = list(gens)
                while gens:
                    for g in list(gens):
                        try:
                            next(g)
                        except StopIteration:
                            gens.remove(g)

            n = len(descs)
            print("phase A sbuf bytes remaining:", nc.sbuf_bytes_remaining)
            for i in range(n + 2):
                gens = []
                if i >= 2:
                    gens.append(chain_ret_b(descs[i - 2]))
                if 1 <= i <= n:
                    dprev = descs[i - 1]
                    gens += [chain_ret(dprev), chain_q(dprev), chain_kv(dprev)]
                if i < n:
                    gens.append(stage1(descs[i]))
                interleave(gens)
            P.finish()
            return P.ninstr

    def phase_B(self, l):
        nc = self.nc
        S, PAST = self.S, self.PAST
        NKS = self.NK_S
        with ExitStack() as st:
            P = Prog(nc, st, f"B{l}")
            sb = lambda shape, dt, name=None: self.sb(st, shape, dt, name)
            PS = self.psum(st)
            w_k = sb([128, 2, 512], BF16, "w_k"); w_v = sb([128, 2, 512], BF16, "w_v")
            self.load_w(P, w_k, self.w_k[l]); self.load_w(P, w_v, self.w_v[l])
            identf = sb([128, 128], F32); ident = sb([128, 128], BF16, "identb")
            P.dma("sp", identf.ap, self.cst[:, 0:128], writes=[identf])
            P.op("dve", I("tensor_copy", out=ident.ap, in_=identf.ap), reads=[identf], writes=[ident])
            onesb = sb([128, 64], BF16, "onesb")
            P.op("dve", I("memset", onesb.ap, 1.0), writes=[onesb])
            rhl = sb([128, 2, 512], BF16, "rhl")
            NKT = S // 128
            NKTS = (NKS + 127) // 128
            Vp = [sb([128, NKT, 128], BF16, "Vp") for _ in range(2)]
            Vs = sb([128, NKTS, 128], BF16, "Vs")
            for v in Vp + [Vs]:
                P.op("pool", I("memset", v.ap[:, :, 64:128], 1.0), writes=[v])
            qtR = Ring([sb([96, 512], BF16, "qt") for _ in range(3)])
            PTr = Ring([sb([128, 512], BF16, "PT") for _ in range(6)])
            rsb = sb([128, 512], F32, "rsb"); rbs = sb([64, 512], F32, "rbs")
            attR = Ring([sb([64, 512], BF16, "att") for _ in range(2)])
            pSr = Ring(PS[0:4]); pOr = Ring(PS[4:6])
            pM = PS[6]; pM2 = PS[7]
            KT = sb([96, 2, S], BF16, "KT"); ckvT = sb([128, 2, S], BF16, "ckvT")
            KTs = sb([96, 1, NKS], BF16, "KTs"); ckvTs = sb([128, 2, NKS], BF16, "ckvTs")
            P.dma("sp", ckvT.ap, self.ckvT_scr.rearrange("(k p) c -> p k c", p=128), writes=[ckvT])
            for cp in range(2):
                P.dma("sp", KT.ap[64:96, cp, :], self.krT_scr, writes=[KT])
            P.dma("sp", ckvTs.ap[:, :, PAST:NKS], self.ckvTs_scr.rearrange("(k p) c -> p k c", p=128), writes=[ckvTs])
            P.dma("sp", KTs.ap[64:96, 0, PAST:NKS], self.krTs_scr, writes=[KTs])

            cbuf = sb([128, PAST // 128, 256], BF16, "cbuf"); kbuf = sb([128, PAST // 128, 96], BF16, "kbuf")
            P.dma("pool", cbuf.ap, self.cckv[l].rearrange("(t p) f -> p t f", p=128), writes=[cbuf])
            P.op("dve", I("memset", kbuf.ap[:, :, 0:64], 0.0), writes=[kbuf])
            P.dma("pool", kbuf.ap[:, :, 64:96], self.ckr[l].rearrange("(t p) f -> p t f", p=128), writes=[kbuf])
            pmR = Ring([pM, pM2])
            for t4 in range(0, PAST // 128, 4):
                nt4 = min(4, PAST // 128 - t4)
                for kc in range(2):
                    pt = pmR.next()
                    ptv = pt.ap.bitcast(BF16).rearrange("p (k c) -> p k c", c=128)
                    P.op("pe", [I("transpose", out=ptv[:, i, :], in_=cbuf.ap[:, t4 + i, kc * 128:(kc + 1) * 128],
                                                                        identity=ident.ap) for i in range(nt4)],
                         reads=[cbuf, ident], writes=[pt])
                    P.op("dve", I("tensor_copy",
                        out=ckvTs.ap[:, kc, t4 * 128:(t4 + nt4) * 128].rearrange("p (k c) -> p k c", c=128),
                        in_=ptv[:, 0:nt4, :]), reads=[pt], writes=[ckvTs])
                pt = pmR.next()
                ptv = pt.ap.bitcast(BF16).rearrange("p (k c) -> p k c", c=128)
                P.op("pe", [I("transpose", out=ptv[:96, i, :], in_=kbuf.ap[:, t4 + i, :], identity=ident.ap)
                            for i in range(nt4)], reads=[kbuf, ident], writes=[pt])
                P.op("dve", I("tensor_copy",
                    out=KTs.ap[64:96, 0, t4 * 128:(t4 + nt4) * 128].rearrange("p (k c) -> p k c", c=128),
                    in_=ptv[64:96, 0:nt4, :]), reads=[pt], writes=[KTs])

            def build_KV(h, KTt, cp, ckvTt, Vt, nkeys):
                for c0 in range(0, nkeys, 512):
                    w = min(512, nkeys - c0)
                    pm = pmR.next()
                    P.op("pe", [I("matmul", pm.ap[0:64, :w], lhsT=w_k.ap[:, kc, h * 64:(h + 1) * 64],
                                                               rhs=ckvTt.ap[:, kc, c0:c0 + w], start=(kc == 0), stop=(kc == 1))
                                for kc in range(2)], reads=[w_k, ckvTt], writes=[pm])
                    P.op("dve", I("tensor_copy", out=KTt.ap[0:64, cp, c0:c0 + w], in_=pm.ap[0:64, :w]),
                         reads=[pm], writes=[KTt])
                nkt = (nkeys + 127) // 128
                for k8 in range(0, nkt, 8):
                    n8 = min(8, nkt - k8)
                    pm = pmR.next()
                    fns = []
                    for i in range(n8):
                        k0 = (k8 + i) * 128
                        nk = min(128, nkeys - k0)
                        for kc in range(2):
                            fns.append(I("matmul",
                                pm.ap[:nk, i * 64:(i + 1) * 64], lhsT=ckvTt.ap[:, kc, k0:k0 + nk],
                                rhs=w_v.ap[:, kc, h * 64:(h + 1) * 64], start=(kc == 0), stop=(kc == 1)))
                    P.op("pe", fns, reads=[ckvTt, w_v], writes=[pm])
                    nkl = min(128, nkeys - (k8 + n8 - 1) * 128)
                    if nkl == 128:
                        P.op("act", I("activation",
                            out=Vt.ap[:, k8:k8 + n8, 0:64], in_=pm.ap[:, 0:n8 * 64].rearrange("p (k d) -> p k d", d=64),
                            func=AF.Copy), reads=[pm], writes=[Vt])
                    else:
                        if n8 > 1:
                            P.op("act", I("activation",
                                out=Vt.ap[:, k8:k8 + n8 - 1, 0:64],
                                in_=pm.ap[:, 0:(n8 - 1) * 64].rearrange("p (k d) -> p k d", d=64), func=AF.Copy),
                                reads=[pm], writes=[Vt])
                        P.op("act", I("activation",
                            out=Vt.ap[:nkl, k8 + n8 - 1, 0:64], in_=pm.ap[:nkl, (n8 - 1) * 64:n8 * 64], func=AF.Copy),
                            reads=[pm], writes=[Vt])

            def load_q(qsrc, W):
                qt = qtR.next()
                P.dma("sp", qt.ap[:, :W], qsrc, writes=[qt])
                return qt

            def attend(h, qt, W, KTt, cp, Vt, ktiles, dst):
                pO = pOr.next()
                n = len(ktiles)
                pend = []

                def issue_S(i):
                    k0, nk, c0, diag = ktiles[i]
                    pS = pSr.next()
                    P.op("pe", I("matmul", pS.ap[:nk, c0:W], lhsT=KTt.ap[0:96, cp, k0:k0 + nk],
                                                         rhs=qt.ap[0:96, c0:W], start=True, stop=True),
                         reads=[KTt, qt], writes=[pS])
                    PTb = PTr.next()
                    P.op("act", I("activation", out=PTb.ap[:nk, c0:W], in_=pS.ap[:nk, c0:W], func=AF.Exp),
                         reads=[pS], writes=[PTb])
                    if diag and nk > 64:
                        P.op("pool", I("memset", PTb.ap[64:nk, c0:c0 + 64], 0.0), writes=[PTb])
                    pend.append((i, PTb))

                def issue_PV():
                    i, PTb = pend.pop(0)
                    k0, nk, c0, diag = ktiles[i]
                    P.op("pe", I("matmul", pO.ap[0:128, c0:W], lhsT=Vt.ap[:nk, k0 // 128, 0:128],
                                                          rhs=PTb.ap[:nk, c0:W], start=(i == 0), stop=(i == n - 1),
                                                          skip_group_check=True),
                         reads=[Vt, PTb], writes=[pO])

                LOOK = 3
                for i in range(n):
                    issue_S(i)
                    if i == min(LOOK, n - 1):
                        flush_pending()
                    if i >= LOOK:
                        issue_PV()
                while pend:
                    issue_PV()
                P.op("dve", I("reciprocal", out=rsb.ap[64:65, :W], in_=pO.ap[64:65, :W]), reads=[pO], writes=[rsb])
                P.op("dve", I("tensor_copy", out=rhl.ap[64:65, 0, :W], in_=rsb.ap[64:65, :W]), reads=[rsb], writes=[rhl])
                P.op("dve", I("tensor_tensor", out=rsb.ap[64:65, :W], in0=rsb.ap[64:65, :W], in1=rhl.ap[64:65, 0, :W],
                              op=ALU.subtract), reads=[rsb, rhl], writes=[rsb])
                P.op("dve", I("tensor_copy", out=rhl.ap[64:65, 1, :W], in_=rsb.ap[64:65, :W]), reads=[rsb], writes=[rhl])

                def fin():
                    pm = pmR.next()
                    P.op("pe", [I("matmul", pm.ap[0:64, :W], lhsT=onesb.ap[64:65, 0:64], rhs=rhl.ap[64:65, i2, :W],
                                  start=(i2 == 0), stop=(i2 == 1)) for i2 in range(2)], reads=[onesb, rhl], writes=[pm])
                    P.op("act", I("activation", out=rbs.ap[:, :W], in_=pm.ap[0:64, :W], func=AF.Copy),
                         reads=[pm], writes=[rbs])
                    at = attR.next()
                    P.op("dve", I("tensor_tensor", out=at.ap[:, :W], in0=pO.ap[0:64, :W], in1=rbs.ap[:, :W], op=ALU.mult),
                         reads=[pO, rbs], writes=[at])
                    P.dma("sp", dst, at.ap[:, :W], reads=[at])
                pending.append(fin)

            pending = []

            def flush_pending():
                while pending:
                    pending.pop(0)()

            work = []
            for h in range(NH):
                cp = h % 2
                for j in range(S // 512):
                    ktiles = []
                    for kt in range(4 * j + 4):
                        c0 = max(0, kt - 4 * j) * 128
                        ktiles.append((kt * 128, 128, c0, kt >= 4 * j))
                    work.append(dict(h=h, first=(j == 0), samp=False, qsrc=self.qT_scr[h, :, j * 512:(j + 1) * 512], W=512,
                                     KT=KT, cp=cp, V=Vp[cp], kt=ktiles,
                                     dst=self.mixT_scr[h * 64:(h + 1) * 64, j * 512:(j + 1) * 512]))
                ktiles = [(k0, min(128, NKS - k0), 0, False) for k0 in range(0, NKS, 128)]
                work.append(dict(h=h, first=True, samp=True, qsrc=self.qTs_scr[h, :, :], W=LS, KT=KTs, cp=0, V=Vs, kt=ktiles,
                                 dst=self.mixTs_scr[h * 64:(h + 1) * 64, :]))
            nxt = load_q(work[0]["qsrc"], work[0]["W"])
            for i, wk in enumerate(work):
                qt = nxt
                if i + 1 < len(work):
                    nxt = load_q(work[i + 1]["qsrc"], work[i + 1]["W"])
                if wk["first"]:
                    if wk["samp"]:
                        build_KV(wk["h"], KTs, 0, ckvTs, Vs, NKS)
                    else:
                        build_KV(wk["h"], KT, wk["cp"], ckvT, Vp[wk["cp"]], S)
                attend(wk["h"], qt, wk["W"], wk["KT"], wk["cp"], wk["V"], wk["kt"], wk["dst"])
            flush_pending()
            P.finish()
            return P.ninstr

    def phase_C1(self, l, hsrc_p, hsrc_s):
        nc = self.nc
        S = self.S
        with ExitStack() as st:
            P = Prog(nc, st, f"C{l}")
            sb = lambda shape, dt, name=None: self.sb(st, shape, dt, name)
            PS = self.psum(st)
            w_out = sb([128, 8, D], BF16, "w_out"); w_ff1 = sb([128, 8, DFF], BF16, "w_ff1"); w_ff2 = sb([128, 32, D], BF16, "w_ff2")
            self.load_w(P, w_out, self.w_out[l]); self.load_w(P, w_ff1, self.w_ff1[l]); self.load_w(P, w_ff2, self.w_ff2[l])
            nw2 = sb([128, D], F32)
            self.load_bc(P, nw2, self.nw2[l:l + 1, :])
            identf = sb([128, 128], F32); ident = sb([128, 128], BF16, "identb")
            P.dma("sp", identf.ap, self.cst[:, 0:128], writes=[identf])
            P.op("dve", I("tensor_copy", out=ident.ap, in_=identf.ap), reads=[identf], writes=[ident])
            TW = 256
            hbR = Ring([sb([128, 2, D], F32, "hb") for _ in range(2)])
            mxR = Ring([sb([128, 8, TW], BF16, "mixT") for _ in range(2)])
            n_bf = sb([128, 2, D], BF16, "n_bf"); nTR = Ring([sb([128, 8, TW], BF16, "nT") for _ in range(2)])
            gT = sb([128, 32, TW], BF16, "gT")
            rlR = Ring([sb([128, TW], F32, "rl") for _ in range(3)])
            junk = sb([128, D], BF16); st4 = sb([128, 8], F32)
            psR = Ring(PS[0:6]); ptR = Ring(PS[6:8])

            def stage1(d):
                hsrc, mixsrc, hdst, t0, ntok = d.args
                nT = nTR.next()
                d.nT = nT
                nsub = (ntok + 127) // 128
                nts = min(128, ntok)
                hb = hbR.next(); mx = mxR.next()
                d.hb = hb
                if nsub == 2:
                    P.dma("sp", hb.ap, hsrc[t0:t0 + ntok, :].rearrange("(s p) d -> p s d", p=128), writes=[hb])
                else:
                    P.dma("sp", hb.ap[:nts, 0, :], hsrc[t0:t0 + ntok, :], writes=[hb])
                P.dma("sp", mx.ap[:, :, :ntok], mixsrc[:, t0:t0 + ntok].rearrange("(k p) c -> p k c", p=128), writes=[mx])
                for s in range(nsub):
                    for c in range(2):
                        ps = psR.next()
                        P.op("pe", [I("matmul", ps.ap[:nts, :], lhsT=mx.ap[:, k, s * 128:s * 128 + nts],
                                                                 rhs=w_out.ap[:, k, c * 512:(c + 1) * 512],
                                                                 start=(k == 0), stop=(k == 7)) for k in range(8)],
                             reads=[mx, w_out], writes=[ps])
                        P.op("dve", I("tensor_tensor", out=hb.ap[:nts, s, c * 512:(c + 1) * 512],
                                                                    in0=hb.ap[:nts, s, c * 512:(c + 1) * 512],
                                                                    in1=ps.ap[:nts, :], op=ALU.add), reads=[hb, ps], writes=[hb])
                        yield
                for s in range(nsub):
                    P.op("act", I("activation", out=junk.ap[:nts, :], in_=hb.ap[:nts, s, :], func=AF.Square,
                                                           accum_out=st4.ap[:nts, s:s + 1]), reads=[hb], writes=[junk, st4])
                self.rstd_from_ss(P, st4, st4, nts, nsub, 1.0 / D)
                for s in range(nsub):
                    P.op("dve", I("scalar_tensor_tensor", out=n_bf.ap[:nts, s, :], in0=hb.ap[:nts, s, :],
                                                                    scalar=st4.ap[:nts, s:s + 1], in1=nw2.ap[:nts, :],
                                                                    op0=ALU.mult, op1=ALU.mult),
                         reads=[hb, st4, nw2], writes=[n_bf])
                    self.transpose_to(P, ident, n_bf, [n_bf.ap[:nts, s, k * 128:(k + 1) * 128] for k in range(8)],
                                      ptR.next(), nT, lambda s=s: nT.ap[:, :, s * 128:s * 128 + nts], 128, nts, "act")
                    yield
                yield

            def stage2(d):
                hsrc, mixsrc, hdst, t0, ntok = d.args
                nsub = (ntok + 127) // 128
                nts = min(128, ntok)
                hb, nT = d.hb, d.nT
                for f in range(32):
                    ps = psR.next()
                    P.op("pe", [I("matmul", ps.ap[:, :ntok], lhsT=w_ff1.ap[:, k, f * 128:(f + 1) * 128],
                                                             rhs=nT.ap[:, k, :ntok], start=(k == 0), stop=(k == 7))
                                for k in range(8)], reads=[w_ff1, nT], writes=[ps])
                    rl = rlR.next()
                    P.op("act", I("activation", out=rl.ap[:, :ntok], in_=ps.ap[:, :ntok], func=AF.Relu),
                         reads=[ps], writes=[rl])
                    eng = "dve"
                    P.op(eng, I("tensor_tensor", out=gT.ap[:, f, :ntok], in0=rl.ap[:, :ntok], in1=rl.ap[:, :ntok],
                                                              op=ALU.mult), reads=[rl], writes=[gT])
                    if f % 4 == 3:
                        yield
                for s in range(nsub):
                    for c in range(2):
                        ps = psR.next()
                        P.op("pe", [I("matmul", ps.ap[:nts, :], lhsT=gT.ap[:, f, s * 128:s * 128 + nts],
                                                                 rhs=w_ff2.ap[:, f, c * 512:(c + 1) * 512],
                                                                 start=(f == 0), stop=(f == 31)) for f in range(32)],
                             reads=[gT, w_ff2], writes=[ps])
                        P.op("dve", I("tensor_tensor", out=hb.ap[:nts, s, c * 512:(c + 1) * 512],
                                                                    in0=hb.ap[:nts, s, c * 512:(c + 1) * 512],
                                                                    in1=ps.ap[:nts, :], op=ALU.add), reads=[hb, ps], writes=[hb])
                        yield
                if nsub == 2:
                    P.dma("sp", hdst[t0:t0 + ntok, :].rearrange("(s p) d -> p s d", p=128), hb.ap, reads=[hb])
                else:
                    P.dma("sp", hdst[t0:t0 + ntok, :], hb.ap[:nts, 0, :], reads=[hb])

            class Desc:
                pass
            descs = []
            for t in range(S // TW):
                if KNT >= 0 and t >= KNT:
                    break
                d = Desc(); d.args = (hsrc_p, self.mixT_scr, self.hbuf, t * TW, TW); descs.append(d)
            d = Desc(); d.args = (hsrc_s, self.mixTs_scr, self.hsbuf, 0, LS); descs.append(d)
            print("phase C1 sbuf bytes remaining:", nc.sbuf_bytes_remaining)

            def interleave(gens):
                gens = list(gens)
                while gens:
                    for g in list(gens):
                        try:
                            next(g)
                        except StopIteration:
                            gens.remove(g)

            n = len(descs)
            for i in range(n + 1):
                gens = []
                if i >= 1:
                    gens.append(stage2(descs[i - 1]))
                if i < n:
                    gens.append(stage1(descs[i]))
                interleave(gens)
            P.finish()
            return P.ninstr

    def phase_C2(self, l, last):
        nc = self.nc
        S = self.S
        with ExitStack() as st:
            P = Prog(nc, st, f"E{l}")
            sb = lambda shape, dt, name=None: self.sb(st, shape, dt, name)
            PS = self.psum(st)
            w_gate = sb([128, 8, D], BF16, "w_gate"); w_proj = sb([128, 2, D], BF16, "w_proj")
            self.load_w(P, w_gate, self.w_gate[l]); self.load_w(P, w_proj, self.w_proj[l])
            nw3 = sb([128, D], F32); fnw = sb([128, D], F32)
            self.load_bc(P, nw3, self.nw3[l:l + 1, :]); self.load_bc(P, fnw, self.fnw[0:1, :])
            identf = sb([128, 128], F32); ident = sb([128, 128], BF16, "identb")
            P.dma("sp", identf.ap, self.cst[:, 0:128], writes=[identf])
            P.op("dve", I("tensor_copy", out=ident.ap, in_=identf.ap), reads=[identf], writes=[ident])
            hbR = Ring([sb([128, D], F32, "hb") for _ in range(5)])
            pbR = Ring([sb([128, 256], BF16, "pb") for _ in range(2)])
            yR = Ring([sb([128, D], F32, "y") for _ in range(2)])
            nbR = Ring([sb([128, D], BF16, "n_bf") for _ in range(2)])
            nTR = Ring([sb([128, 8, 128], BF16, "nT") for _ in range(2)])
            ppTR = Ring([sb([128, 2, 128], BF16, "ppT") for _ in range(2)])
            junk = sb([128, D], BF16); st1 = sb([128, 4], F32); stf = sb([128, 4], F32)
            egR = Ring([sb([128, 512], F32, "eg") for _ in range(3)])
            psR = Ring(PS[0:6]); ptR = Ring(PS[6:8])

            class Desc:
                pass

            def stage1(d):
                nt, t0 = d.nt, d.t0
                hb = hbR.next(); pb = pbR.next(); n_bf = nbR.next(); nT = nTR.next(); ppT = ppTR.next()
                d.hb, d.nT, d.ppT = hb, nT, ppT
                P.dma("sp", hb.ap[:nt, :], d.hsrc[t0:t0 + nt, :], writes=[hb])
                P.dma("pool", pb.ap[:nt, :], d.psrc[t0:t0 + nt, :], writes=[pb])
                yield
                P.op("act", I("activation", out=junk.ap[:nt, :], in_=hb.ap[:nt, :], func=AF.Square,
                              accum_out=st1.ap[:nt, 0:1]), reads=[hb], writes=[junk, st1])
                self.rstd_from_ss(P, st1, st1, nt, 1, 1.0 / D)
                yield
                P.op("dve", I("scalar_tensor_tensor", out=n_bf.ap[:nt, :], in0=hb.ap[:nt, :], scalar=st1.ap[:nt, 0:1],
                              in1=nw3.ap[:nt, :], op0=ALU.mult, op1=ALU.mult), reads=[hb, st1, nw3], writes=[n_bf])
                yield
                self.transpose_to(P, ident, n_bf, [n_bf.ap[:nt, k * 128:(k + 1) * 128] for k in range(8)],
                                  ptR.next(), nT, lambda: nT.ap[:, :, :nt], 128, nt, "act")
                yield
                self.transpose_to(P, ident, pb, [pb.ap[:nt, k * 128:(k + 1) * 128] for k in range(2)],
                                  ptR.next(), ppT, lambda: ppT.ap[:, :, :nt], 128, nt, "act")
                yield

            def stage2(d):
                nt, t0, hb, nT, ppT = d.nt, d.t0, d.hb, d.nT, d.ppT
                for c in range(2):
                    pg = psR.next(); pq = psR.next()
                    P.op("pe", [I("matmul", pg.ap[:nt, :], lhsT=nT.ap[:, k, :nt], rhs=w_gate.ap[:, k, c * 512:(c + 1) * 512],
                                  start=(k == 0), stop=(k == 7)) for k in range(8)], reads=[nT, w_gate], writes=[pg])
                    P.op("pe", [I("matmul", pq.ap[:nt, :], lhsT=ppT.ap[:, k, :nt], rhs=w_proj.ap[:, k, c * 512:(c + 1) * 512],
                                  start=(k == 0), stop=(k == 1)) for k in range(2)], reads=[ppT, w_proj], writes=[pq])
                    yield
                    eg = egR.next()
                    P.op("act", I("activation", out=eg.ap[:nt, :], in_=pg.ap[:nt, :], func=AF.Exp, scale=-1.0),
                         reads=[pg], writes=[eg])
                    yield
                    P.op("act", I("activation", out=eg.ap[:nt, :], in_=eg.ap[:nt, :], func=AF.Ln, scale=1.0, bias=1.0),
                         reads=[eg], writes=[eg])
                    P.op("act", I("activation", out=eg.ap[:nt, :], in_=eg.ap[:nt, :], func=AF.Exp, scale=-1.0),
                         reads=[eg], writes=[eg])
                    yield
                    P.op("dve", I("tensor_tensor", out=eg.ap[:nt, :], in0=eg.ap[:nt, :], in1=pq.ap[:nt, :], op=ALU.mult),
                         reads=[eg, pq], writes=[eg])
                    yield
                    P.op("dve", I("tensor_tensor", out=hb.ap[:nt, c * 512:(c + 1) * 512], in0=hb.ap[:nt, c * 512:(c + 1) * 512],
                                  in1=eg.ap[:nt, :], op=ALU.add), reads=[hb, eg], writes=[hb])
                    yield
                yield

            def stage3(d):
                nt, t0, hb = d.nt, d.t0, d.hb
                if last:
                    P.op("act", I("activation", out=junk.ap[:nt, :], in_=hb.ap[:nt, :], func=AF.Square,
                                  accum_out=stf.ap[:nt, 0:1]), reads=[hb], writes=[junk, stf])
                    self.rstd_from_ss(P, stf, stf, nt, 1, 1.0 / D)
                    yield
                    yb = yR.next()
                    P.op("dve", I("scalar_tensor_tensor", out=yb.ap[:nt, :], in0=hb.ap[:nt, :], scalar=stf.ap[:nt, 0:1],
                                  in1=fnw.ap[:nt, :], op0=ALU.mult, op1=ALU.mult), reads=[hb, stf, fnw], writes=[yb])
                    yield
                    P.dma("sp", d.ydst[t0:t0 + nt, :], yb.ap[:nt, :], reads=[yb])
                else:
                    P.dma("sp", d.hdst[t0:t0 + nt, :], hb.ap[:nt, :], reads=[hb])
                yield

            descs = []
            for t in range(S // 128 + 1):
                if KNT >= 0 and KNT <= t < S // 128:
                    continue
                d = Desc()
                is_s = (t == S // 128)
                d.nt = LS if is_s else 128
                d.t0 = 0 if is_s else t * 128
                d.hsrc = self.hsbuf if is_s else self.hbuf
                d.hdst = d.hsrc
                d.psrc = self.pps[l] if is_s else self.pp[l]
                d.ydst = self.ys if is_s else self.y
                descs.append(d)

            def interleave(gens):
                gens = list(gens)
                while gens:
                    for g in list(gens):
                        try:
                            next(g)
                        except StopIteration:
                            gens.remove(g)

            n = len(descs)
            for i in range(n + 2):
                gens = []
                if i >= 2:
                    gens.append(stage3(descs[i - 2]))
                if 1 <= i <= n:
                    gens.append(stage2(descs[i - 1]))
                if i < n:
                    gens.append(stage1(descs[i]))
                interleave(gens)
            P.finish()
            return P.ninstr

    def build(self):
        nc = self.nc
        S, PAST = self.S, self.PAST
        tot = 0
        import os
        ph = os.environ.get("KPH", "A0B0C0E0A1B1C1E1")
        for l in range(2):
            hp = self.x if l == 0 else self.hbuf
            hs = self.xs if l == 0 else self.hsbuf
            if f"A{l}" in ph:
                tot += self.phase_A(l, hp, hs, 128)
            if f"B{l}" in ph:
                tot += self.phase_B(l)
            if f"C{l}" in ph:
                tot += self.phase_C1(l, hp, hs)
            if f"E{l}" in ph:
                tot += self.phase_C2(l, l == 1 or f"E{l+1}" not in ph)
        self.ninstr = tot
        return nc


_CACHE = {}


def _swap16(a):
    return np.concatenate([a[..., 16:], a[..., :16]], axis=-1)


def _consts(S, PAST):
    half = 16
    inv = (10000.0 ** (-np.arange(half, dtype=np.float32) / half)).astype(np.float32)

    def tabs(pos):
        ang = pos.astype(np.float32)[:, None] * inv[None, :]
        c, s = np.cos(ang).astype(np.float32), np.sin(ang).astype(np.float32)
        cs = np.concatenate([c, c], -1)
        sn = np.concatenate([-s, s], -1)
        return np.concatenate([cs, sn, cs * np.float32(MLA_SCALE), sn * np.float32(MLA_SCALE)], -1).astype(np.float32)

    tab_p = tabs(np.arange(S))
    tab_s = tabs(PAST + np.arange(LS))
    lg = np.log1p(-np.exp2(-5.0 - np.arange(NH, dtype=np.float64)))
    n = np.arange(128, dtype=np.float64)
    dq = np.exp((n[:, None] + 1.0) * lg[None, :])
    dk = np.exp(-(n[:, None] + 1.0) * lg[None, :]) * (32 ** -0.5)
    cst = np.zeros((128, 2176), np.float32)
    cst[:, 0:128] = np.eye(128)
    cst[:, 128:256] = (np.arange(128)[None, :] >= np.arange(128)[:, None])
    p = np.arange(128)
    bm = (p[:, None] // 32 == np.arange(4)[None, :]).astype(np.float32)
    cst[:, 256:768] = np.repeat(bm, 128, axis=1)
    cst[:, 768:1024] = np.repeat(bm, 64, axis=1)
    cst[:, 1024:1280] = np.repeat(dq, 32, axis=1)
    cst[:, 1280:1536] = np.repeat(dk, 32, axis=1)
    for (L, c0) in ((128, 2048), (LS, 2050)):
        for g in range(2):
            cst[:, c0 + g] = np.exp(L * lg[4 * g + p // 32])
    return tab_p, tab_s, cst


def _prep_weights(w_in, w_uq, w_ukv):
    q_lat = w_in[:, :, 0:384]; ckv = w_in[:, :, 384:640]; kr = w_in[:, :, 640:672]
    rq = w_in[:, :, 672:928].reshape(2, D, NH, 32); rk = w_in[:, :, 928:1184].reshape(2, D, NH, 32)
    rv = w_in[:, :, 1184:1696]; rg = w_in[:, :, 1696:2208]
    rq2 = np.concatenate([rq, _swap16(rq)], -1).reshape(2, D, 512)
    rk2 = np.concatenate([rk, _swap16(rk)], -1).reshape(2, D, 512)
    w_in_p = np.concatenate([q_lat, kr, _swap16(kr), ckv, rq2, rk2, rv, rg], -1)
    assert w_in_p.shape[-1] == NZ
    uq = w_uq.reshape(2, 384, NH, 96)
    w_uq_p = np.concatenate([uq[..., 64:96], _swap16(uq[..., 64:96]), uq[..., 0:64]], -1).reshape(2, 384, 1024)
    ukv = w_ukv.reshape(2, 256, NH, 128)
    w_k = ukv[..., 0:64].reshape(2, 256, 512)
    w_v = ukv[..., 64:128].reshape(2, 256, 512)
    c = np.ascontiguousarray
    return c(w_in_p), c(w_uq_p), c(w_k), c(w_v)


def kernel(x_prompt, x_sample, cache_ckv, cache_krope, state_ret, p_prompt, p_sample,
           norm_mix_w, w_in, q_norm_w, w_uq, kv_norm_w, w_ukv, ret_gn_w, w_out,
           norm_ffn_w, w_ff1, w_ff2, norm_ple_w, w_ple_gate, w_ple_proj, final_norm_w):
    f = lambda a: np.ascontiguousarray(np.asarray(a, dtype=np.float32))
    x_prompt, x_sample, cache_ckv, cache_krope, state_ret, p_prompt, p_sample = map(
        f, (x_prompt, x_sample, cache_ckv, cache_krope, state_ret, p_prompt, p_sample))
    B, S, _ = x_prompt.shape
    Bd = x_sample.shape[0]
    PAST = cache_ckv.shape[2]
    key = (S, PAST)
    if key not in _CACHE:
        _CACHE[key] = Builder(S, PAST).build()
    nc = _CACHE[key]
    tab_p, tab_s, cst = _consts(S, PAST)
    w_in_p, w_uq_p, w_k, w_v = _prep_weights(f(w_in), f(w_uq), f(w_ukv))
    shared = dict(w_in=w_in_p, w_uq=w_uq_p, w_k=w_k, w_v=w_v, w_out=f(w_out), w_ff1=f(w_ff1), w_ff2=f(w_ff2),
                  w_gate=f(w_ple_gate), w_proj=f(w_ple_proj), nw1=f(norm_mix_w), qnw=f(q_norm_w), kvnw=f(kv_norm_w),
                  gnw=f(ret_gn_w), nw2=f(norm_ffn_w), nw3=f(norm_ple_w), fnw=f(final_norm_w).reshape(1, D),
                  tab_p=tab_p, tab_s=tab_s, cst=cst)
    ncores = 8
    in_maps = []
    for c in range(ncores):
        b = (c // 2) % B
        sidx = c % Bd
        m = dict(shared)
        m.update(x=x_prompt[b], xs=x_sample[sidx], cckv=f(cache_ckv[:, sidx]), ckr=f(cache_krope[:, sidx]),
                 sret=f(state_ret[:, sidx]), pp=f(p_prompt[:, b]), pps=f(p_sample[:, sidx]))
        in_maps.append(m)
    res = run_bass_kernel_spmd(nc, in_maps, core_ids=list(range(ncores)))
    r = res.results
    global LAST_RES
    LAST_RES = r
    pc = [2 * b for b in range(B)]
    y_prompt = np.stack([r[c]["y"] for c in pc])
    ckv_prompt = np.stack([r[c]["ckv_o"] for c in pc], axis=1)
    krope_prompt = np.stack([r[c]["kr_o"] for c in pc], axis=1)
    ret_prompt = np.stack([r[c]["ret_o"] for c in pc], axis=1)
    y_sample = np.stack([r[c]["ys"] for c in range(Bd)])
    ckv_sample = np.stack([r[c]["ckvs_o"] for c in range(Bd)], axis=1)
    krope_sample = np.stack([r[c]["krs_o"] for c in range(Bd)], axis=1)
    ret_sample = np.stack([r[c]["rets_o"] for c in range(Bd)], axis=1)
    out = (y_prompt, y_sample, ckv_prompt, krope_prompt, ret_prompt, ckv_sample, krope_sample, ret_sample)
    return tuple(np.ascontiguousarray(o, dtype=np.float32) for o in out)
```

```python
import numpy as np
from contextlib import ExitStack
import concourse.bass as bass
import concourse.mybir as mybir
from concourse.bass_utils import run_bass_kernel_spmd

F32 = mybir.dt.float32
BF16 = mybir.dt.bfloat16
AF = mybir.ActivationFunctionType
ALU = mybir.AluOpType
AX = mybir.AxisListType

D = 1024
DFF = 4096
NH = 8
LS = 32
EPS = 1e-6
MLA_SCALE = 96 ** -0.5
NZ = 2752
import os as _os
KNT = int(_os.environ.get('KNT', '-1'))
ZOFF = [0, 448, 704, 1216, 1728, 2240, 2752]


def I(name, *a, **kw):
    return lambda e: getattr(e, name)(*a, **kw)


class Buf:
    __slots__ = ("name", "w", "r", "sem", "cnt")

    def __init__(self, name):
        self.name = name
        self.w = None
        self.r = []
        self.sem = None
        self.cnt = 0


class T:
    __slots__ = ("ap", "b")

    def __init__(self, ap, name):
        self.ap = ap
        self.b = Buf(name)


class Prog:
    ENGS = ("pe", "act", "dve", "pool", "sp")

    def __init__(self, nc, stack, tag):
        self.nc = nc
        self.stack = stack
        self.tag = tag
        self.ops = {e: [] for e in self.ENGS}
        self.sem = {}
        self.cnt = {e: 0 for e in self.ENGS}
        self.waited = {e: {} for e in self.ENGS}
        for e in ("pe", "act", "dve", "pool"):
            self.sem[e] = nc.alloc_semaphore(name=f"s{tag}_{e}")
        self.dma_bufs = []
        self.ninstr = 0

    def _wait(self, eng, tok):
        if tok is None:
            return
        sem, val, src = tok
        key = id(sem)
        if self.waited[eng].get(key, 0) >= val:
            return
        self.waited[eng][key] = val
        self.ops[eng].append(lambda e, sem=sem, val=val: e.wait_ge(sem, val))

    def _deps(self, eng, reads, writes):
        for b in reads:
            self._wait(eng, b.w)
        for b in writes:
            if b.w is not None and b.w[2] != eng:
                self._wait(eng, b.w)
            for t in b.r:
                if t[2] != eng:
                    self._wait(eng, t)

    def _mark(self, tok, reads, writes):
        for b in reads:
            b.r.append(tok)
            if len(b.r) > 12:
                last = {}
                for t in b.r:
                    k = id(t[0])
                    if k not in last or last[k][1] < t[1]:
                        last[k] = t
                b.r = list(last.values())
        for b in writes:
            b.w = tok
            b.r = []

    def op(self, eng, fns, reads=(), writes=()):
        if callable(fns):
            fns = [fns]
        reads = [x.b if isinstance(x, T) else x for x in reads]
        writes = [x.b if isinstance(x, T) else x for x in writes]
        self._deps(eng, reads, writes)
        self.cnt[eng] += 1
        sem = self.sem[eng]
        tok = (sem, self.cnt[eng], eng)
        for f in fns[:-1]:
            self.ops[eng].append(f)
        last = fns[-1]
        self.ops[eng].append(lambda e, last=last, sem=sem: last(e).then_inc(sem, 1))
        self.ninstr += len(fns)
        self._mark(tok, reads, writes)
        return tok

    def dma(self, q, out, in_, reads=(), writes=()):
        reads = [x.b if isinstance(x, T) else x for x in reads]
        writes = [x.b if isinstance(x, T) else x for x in writes]
        owner = writes[0] if writes else reads[0]
        if owner.sem is None:
            owner.sem = self.nc.alloc_semaphore(name=f"d{self.tag}_{len(self.dma_bufs)}")
            self.dma_bufs.append(owner)
        self._deps(q, reads, writes)
        owner.cnt += 16
        sem = owner.sem
        tok = (sem, owner.cnt, None)
        self.ops[q].append(lambda e, out=out, in_=in_, sem=sem: e.dma_start(out=out, in_=in_).then_inc(sem, 16))
        self.ninstr += 1
        self._mark(tok, reads, writes)
        return tok

    def finish(self):
        for e in self.ENGS:
            for s in ("pe", "act", "dve", "pool"):
                if s != e and self.cnt[s]:
                    self._wait(e, (self.sem[s], self.cnt[s], s))
            for b in self.dma_bufs:
                self._wait(e, (b.sem, b.cnt, None))
        allsems = [self.sem[s] for s in ("pe", "act", "dve", "pool")] + [b.sem for b in self.dma_bufs]
        for b in self.dma_bufs:
            b.sem = None
            b.cnt = 0
            b.w = None
            b.r = []
        ops = self.ops
        with self.nc.Block() as block:
            @block.tensor
            def _(e):
                for f in ops["pe"]:
                    f(e)

            @block.scalar
            def _(e):
                for f in ops["act"]:
                    f(e)

            @block.vector
            def _(e):
                for f in ops["dve"]:
                    f(e)

            @block.gpsimd
            def _(e):
                for f in ops["pool"]:
                    f(e)

            @block.sync
            def _(e):
                for f in ops["sp"]:
                    f(e)
        self.nc.all_engine_barrier()
        self.nc.clear_and_free_semaphores(allsems)
        self.nc.all_engine_barrier()


class Ring:
    def __init__(self, items):
        self.items = items
        self.i = 0

    def next(self):
        x = self.items[self.i % len(self.items)]
        self.i += 1
        return x


def bc_mid(ap, n):
    p, f = ap.shape
    return ap.unsqueeze(1).to_broadcast([p, n, f])


def bc_last(ap, n):
    p, k = ap.shape
    return ap.unsqueeze(2).to_broadcast([p, k, n])


class Builder:
    def __init__(self, S, PAST):
        self.S, self.PAST = S, PAST
        self.NK_S = PAST + LS
        nc = bass.Bass("TRN2", target_bir_lowering=False)
        self.nc = nc
        di = lambda n, s, dt=F32: nc.dram_tensor(n, s, dt, kind="ExternalInput").ap()
        do = lambda n, s: nc.dram_tensor(n, s, F32, kind="ExternalOutput").ap()
        import os
        dbg = os.environ.get("KDBG") == "1"
        ds = lambda n, s, dt: nc.dram_tensor(n, s, dt, kind=("ExternalOutput" if dbg else "Internal")).ap()
        self.x = di("x", [S, D]); self.xs = di("xs", [LS, D])
        self.cckv = di("cckv", [2, PAST, 256]); self.ckr = di("ckr", [2, PAST, 32])
        self.sret = di("sret", [2, NH, 32, 64])
        self.pp = di("pp", [2, S, 256]); self.pps = di("pps", [2, LS, 256])
        self.w_in = di("w_in", [2, D, NZ]); self.w_uq = di("w_uq", [2, 384, 1024])
        self.w_k = di("w_k", [2, 256, 512]); self.w_v = di("w_v", [2, 256, 512])
        self.w_out = di("w_out", [2, D, D]); self.w_ff1 = di("w_ff1", [2, D, DFF])
        self.w_ff2 = di("w_ff2", [2, DFF, D]); self.w_gate = di("w_gate", [2, D, D])
        self.w_proj = di("w_proj", [2, 256, D])
        self.nw1 = di("nw1", [2, D]); self.qnw = di("qnw", [2, 384]); self.kvnw = di("kvnw", [2, 256])
        self.gnw = di("gnw", [2, 512]); self.nw2 = di("nw2", [2, D]); self.nw3 = di("nw3", [2, D])
        self.fnw = di("fnw", [1, D])
        self.tab_p = di("tab_p", [S, 128]); self.tab_s = di("tab_s", [LS, 128])
        self.cst = di("cst", [128, 2176])
        self.y = do("y", [S, D]); self.ys = do("ys", [LS, D])
        self.ckv_o = do("ckv_o", [2, S, 256]); self.kr_o = do("kr_o", [2, S, 32])
        self.ret_o = do("ret_o", [2, NH, 32, 64])
        self.ckvs_o = do("ckvs_o", [2, LS, 256]); self.krs_o = do("krs_o", [2, LS, 32])
        self.rets_o = do("rets_o", [2, NH, 32, 64])
        self.qT_scr = ds("qT_scr", [NH, 96, S], BF16); self.qTs_scr = ds("qTs_scr", [NH, 96, LS], BF16)
        self.mixT_scr = ds("mixT_scr", [D, S], BF16); self.mixTs_scr = ds("mixTs_scr", [D, LS], BF16)
        self.hbuf = ds("hbuf", [S, D], F32); self.hsbuf = ds("hsbuf", [LS, D], F32)
        self.ckvT_scr = ds("ckvT_scr", [256, S], BF16); self.ckvTs_scr = ds("ckvTs_scr", [256, LS], BF16)
        self.krT_scr = ds("krT_scr", [32, S], BF16); self.krTs_scr = ds("krTs_scr", [32, LS], BF16)
        self.uid = 0

    def sb(self, st, shape, dt, name=None):
        self.uid += 1
        nm = f"{name or 't'}_{self.uid}"
        t = st.enter_context(self.nc.sbuf_tensor(nm, list(shape), dt))
        return T(t[:], nm)

    def psum(self, st):
        banks = []
        for i in range(8):
            self.uid += 1
            t = st.enter_context(self.nc.psum_tensor(f"ps{self.uid}", [128, 512], F32))
            banks.append(T(t[:], f"ps{i}"))
        return banks

    def load_w(self, P, dst, src_kpn):
        K = dst.ap.shape[1]
        N = dst.ap.shape[2]
        step = max(1, 8192 // N)
        for k0 in range(0, K, step):
            k1 = min(K, k0 + step)
            P.dma("pool", dst.ap[:, k0:k1, :],
                  src_kpn[k0 * 128:k1 * 128, :].rearrange("(k p) n -> p k n", p=128), writes=[dst])

    def load_bc(self, P, dst, src_row):
        n = src_row.shape[1]
        P.dma("sp", dst.ap, src_row.to_broadcast([128, n]), writes=[dst])

    def rstd_from_ss(self, P, ss, rstd, n, cols, inv_d):
        P.op("act", I("activation", out=rstd.ap[:n, :cols], in_=ss.ap[:n, :cols], func=AF.Ln,
                                           scale=inv_d, bias=EPS), reads=[ss], writes=[rstd])
        P.op("act", I("activation", out=rstd.ap[:n, :cols], in_=rstd.ap[:n, :cols], func=AF.Exp,
                                           scale=-0.5), reads=[rstd], writes=[rstd])

    def transpose_to(self, P, ident, src, src_aps, pt, dst, dst_ap_fn, npart_out, n, copy_eng="dve"):
        m = len(src_aps)
        ptv = pt.ap.bitcast(BF16).rearrange("p (k c) -> p k c", c=128)
        P.op("pe", [I("transpose", out=ptv[:a.shape[1], i, :n], in_=a, identity=ident.ap[:n, :n])
                    for i, a in enumerate(src_aps)], reads=[src, ident], writes=[pt])
        dap = dst_ap_fn()
        if copy_eng == "act":
            P.op("act", I("activation", out=dap, in_=ptv[:npart_out, :m, :n], func=AF.Copy),
                 reads=[pt], writes=[dst])
        else:
            P.op(copy_eng, I("tensor_copy", out=dap, in_=ptv[:npart_out, :m, :n]), reads=[pt], writes=[dst])

    def phase_A(self, l, hsrc_p, hsrc_s, L):
        nc = self.nc
        S, PAST = self.S, self.PAST
        with ExitStack() as st:
            P = Prog(nc, st, f"A{l}")
            sb = lambda shape, dt, name=None: self.sb(st, shape, dt, name)
            PS = self.psum(st)
            w_in = sb([128, 8, NZ], BF16, "w_in"); w_uq = sb([128, 3, 1024], BF16, "w_uq")
            self.load_w(P, w_in, self.w_in[l]); self.load_w(P, w_uq, self.w_uq[l])
            nw1 = sb([128, D], F32); qnw = sb([128, 384], F32); kvnw = sb([128, 256], F32); gnw = sb([128, 512], F32)
            self.load_bc(P, nw1, self.nw1[l:l + 1, :]); self.load_bc(P, qnw, self.qnw[l:l + 1, :])
            self.load_bc(P, kvnw, self.kvnw[l:l + 1, :]); self.load_bc(P, gnw, self.gnw[l:l + 1, :])
            cstf = sb([128, 2176], F32, "cstf")
            P.dma("sp", cstf.ap, self.cst, writes=[cstf])
            cstb = sb([128, 768], BF16, "cstb")
            P.op("dve", I("tensor_copy", out=cstb.ap, in_=cstf.ap[:, 0:768]), reads=[cstf], writes=[cstb])
            ident = T(cstb.ap[:, 0:128], "ident"); ident.b = cstb.b
            causal = cstb.ap[:, 128:256]
            bmq = cstb.ap[:, 256:768].rearrange("p (j n) -> p j n", j=4)
            bms = cstf.ap[:, 768:1024]
            Dq = cstf.ap[:, 1024:1280]; Dk = cstf.ap[:, 1280:1536]
            gl2 = {128: cstf.ap[:, 2048:2050], 32: cstf.ap[:, 2050:2052]}

            hbR = Ring([sb([128, D], F32, "hb") for _ in range(2)])
            tbR = Ring([sb([128, 128], F32, "tb") for _ in range(3)])
            zR = Ring([sb([128, NZ], F32, "z") for _ in range(3)])
            osbR = Ring([sb([128, 512], F32, "o_sb") for _ in range(2)])
            junk = sb([128, D], BF16, "junk")
            st1 = sb([128, 4], F32, "st1"); stq = sb([128, 4], F32, "stq"); stk = sb([128, 4], F32, "stk")
            a_bf = sb([128, D], BF16, "a_bf"); aT = sb([128, 8, 128], BF16, "aT")
            qn_bf = sb([128, 384], BF16); qnT = sb([128, 3, 128], BF16)
            t1q = sb([128, 256], F32, "t1q"); t2q = sb([128, 256], F32, "t2q")
            t1k = sb([128, 32], F32, "t1k"); t2k = sb([128, 32], F32, "t2k")
            t1r = sb([128, 256], F32, "t1r"); t2r = sb([128, 256], F32, "t2r")
            q_bf = sb([128, NH, 96], BF16, "q_bf")
            ckvn = Ring([sb([128, 256], F32, "ckvn") for _ in range(2)])
            ckvn_bf = sb([128, 256], BF16)
            krf = Ring([sb([128, 32], F32, "krf") for _ in range(2)])
            kr_bf = sb([128, 32], BF16)
            qd_bf = sb([128, 256], BF16); kd_bf = sb([128, 256], BF16); v_bf = sb([128, 512], BF16)
            qkT = sb([128, 4, 128], BF16, "qkT"); qbd = sb([128, 2, 4, 128], BF16, "qbd")
            innerT = sb([128, NH, 128], BF16, "innerT")
            osq = sb([128, 512], F32); on = sb([128, 512], F32); eg = sb([128, 512], F32)
            gst = sb([128, 64], F32, "gst")
            mixr = sb([128, 512], BF16)
            stmp = sb([128, 512], F32, "stmp")
            qTst = [sb([96, NH, 512], BF16, "qTst") for _ in range(2)]
            mixTst = [sb([128, 4, 512], BF16, "mixTst") for _ in range(2)]
            ckvTst = [sb([128, 2, 512], BF16, "ckvTst") for _ in range(2)]
            krTst = [sb([32, 512], BF16, "krTst") for _ in range(2)]
            ptR = Ring([PS[6], PS[7]])

            states = {}
            for is_s in (False, True):
                Sst = sb([128, 2, 256], F32, "Sst"); Sbf = sb([128, 2, 256], BF16, "Sbf")
                P.op("dve", I("memset", Sst.ap, 0.0), writes=[Sst])
                if is_s:
                    for h in range(NH):
                        g, j = h // 4, h % 4
                        P.dma("sp", Sst.ap[32 * j:32 * j + 32, g, j * 64:(j + 1) * 64], self.sret[l, h], writes=[Sst])
                P.op("dve", I("tensor_copy", out=Sbf.ap, in_=Sst.ap), reads=[Sst], writes=[Sbf])
                states[is_s] = (Sst, Sbf)

            class Desc:
                pass
            descs = []
            ntp = S // 128
            for t in range(ntp + 1):
                d = Desc()
                d.is_s = (t == ntp)
                d.nt = LS if d.is_s else 128
                d.t0 = 0 if d.is_s else t * 128
                d.tt = 0 if d.is_s else t % 4
                d.sel = (t // 4) % 2
                d.flush = d.is_s or d.tt == 3 or t == ntp - 1
                d.c0 = 0 if d.is_s else (t // 4) * 512
                d.last = d.is_s or t == ntp - 1
                d.hsrc = hsrc_s if d.is_s else hsrc_p
                d.tab = self.tab_s if d.is_s else self.tab_p
                d.ckv_o = self.ckvs_o if d.is_s else self.ckv_o
                d.kr_o = self.krs_o if d.is_s else self.kr_o
                d.ret_o = self.rets_o if d.is_s else self.ret_o
                d.ckvT_scr = self.ckvTs_scr if d.is_s else self.ckvT_scr
                d.krT_scr = self.krTs_scr if d.is_s else self.krT_scr
                d.qT_scr = self.qTs_scr if d.is_s else self.qT_scr
                d.mixT_scr = self.mixTs_scr if d.is_s else self.mixT_scr
                d.Lblk = LS if d.is_s else 128
                d.Sst, d.Sbf = states[d.is_s]
                descs.append(d)

            def stage1(d):
                nt, t0 = d.nt, d.t0
                hb = hbR.next(); tb = tbR.next(); z = zR.next()
                d.tb, d.z = tb, z
                P.dma("sp", hb.ap[:nt, :], d.hsrc[t0:t0 + nt, :], writes=[hb])
                P.dma("sp", tb.ap[:nt, :], d.tab[t0:t0 + nt, :], writes=[tb])
                yield
                P.op("act", I("activation", out=junk.ap[:nt, :], in_=hb.ap[:nt, :], func=AF.Square,
                              accum_out=st1.ap[:nt, 0:1]), reads=[hb], writes=[junk, st1])
                self.rstd_from_ss(P, st1, st1, nt, 1, 1.0 / D)
                yield
                P.op("dve", I("scalar_tensor_tensor", out=a_bf.ap[:nt, :], in0=hb.ap[:nt, :], scalar=st1.ap[:nt, 0:1],
                              in1=nw1.ap[:nt, :], op0=ALU.mult, op1=ALU.mult), reads=[hb, st1, nw1], writes=[a_bf])
                yield
                self.transpose_to(P, ident, a_bf, [a_bf.ap[:nt, k * 128:(k + 1) * 128] for k in range(8)],
                                  ptR.next(), aT, lambda: aT.ap[:, :, :nt], 128, nt, "dve")
                yield
                for ps_ in range(2):
                    fns = []
                    for k in range(8):
                        for n in range(3):
                            zi = ps_ * 3 + n
                            w = ZOFF[zi + 1] - ZOFF[zi]
                            fns.append(I("matmul", PS[n].ap[:nt, :w], lhsT=aT.ap[:, k, :nt],
                                         rhs=w_in.ap[:, k, ZOFF[zi]:ZOFF[zi] + w], start=(k == 0), stop=(k == 7)))
                    P.op("pe", fns, reads=[aT, w_in], writes=PS[0:3])
                    yield
                    for n in range(3):
                        zi = ps_ * 3 + n
                        w = ZOFF[zi + 1] - ZOFF[zi]
                        if n % 2 == 0:
                            P.op("act", I("activation", out=z.ap[:nt, ZOFF[zi]:ZOFF[zi] + w], in_=PS[n].ap[:nt, :w],
                                          func=AF.Copy), reads=[PS[n]], writes=[z])
                        else:
                            P.op("dve", I("tensor_copy", out=z.ap[:nt, ZOFF[zi]:ZOFF[zi] + w], in_=PS[n].ap[:nt, :w]),
                                 reads=[PS[n]], writes=[z])
                        yield

            def chain_q(d):
                nt, t0, tt, z, tb = d.nt, d.t0, d.tt, d.z, d.tb
                P.op("act", I("activation", out=junk.ap[:nt, 0:384], in_=z.ap[:nt, 0:384], func=AF.Square,
                              accum_out=stq.ap[:nt, 0:1]), reads=[z], writes=[junk, stq])
                self.rstd_from_ss(P, stq, stq, nt, 1, 1.0 / 384)
                yield
                P.op("dve", I("scalar_tensor_tensor", out=qn_bf.ap[:nt, :], in0=z.ap[:nt, 0:384], scalar=stq.ap[:nt, 0:1],
                              in1=qnw.ap[:nt, :], op0=ALU.mult, op1=ALU.mult), reads=[z, stq, qnw], writes=[qn_bf])
                yield
                self.transpose_to(P, ident, qn_bf, [qn_bf.ap[:nt, k * 128:(k + 1) * 128] for k in range(3)],
                                  ptR.next(), qnT, lambda: qnT.ap[:, :, :nt], 128, nt, "act")
                yield
                fns = []
                for k in range(3):
                    for c in range(2):
                        fns.append(I("matmul", PS[3 + c].ap[:nt, :], lhsT=qnT.ap[:, k, :nt],
                                     rhs=w_uq.ap[:, k, c * 512:(c + 1) * 512], start=(k == 0), stop=(k == 2)))
                P.op("pe", fns, reads=[qnT, w_uq], writes=PS[3:5])
                yield
                for c in range(2):
                    qv = PS[3 + c].ap[:nt, :].rearrange("p (j d) -> p j d", d=128)
                    t1v = t1q.ap[:nt, c * 128:(c + 1) * 128].rearrange("p (j d) -> p j d", d=32)
                    t2v = t2q.ap[:nt, c * 128:(c + 1) * 128].rearrange("p (j d) -> p j d", d=32)
                    P.op("dve", I("tensor_tensor", out=t1v, in0=qv[:, :, 0:32], in1=bc_mid(tb.ap[:nt, 64:96], 4), op=ALU.mult),
                         reads=[PS[3 + c], tb], writes=[t1q])
                    P.op("dve", I("tensor_tensor", out=t2v, in0=qv[:, :, 32:64], in1=bc_mid(tb.ap[:nt, 96:128], 4), op=ALU.mult),
                         reads=[PS[3 + c], tb], writes=[t2q])
                    yield
                    P.op("dve", I("tensor_tensor", out=q_bf.ap[:nt, c * 4:(c + 1) * 4, 64:96], in0=t1v, in1=t2v, op=ALU.add),
                         reads=[t1q, t2q], writes=[q_bf])
                    P.op("act", I("activation", out=q_bf.ap[:nt, c * 4:(c + 1) * 4, 0:64], in_=qv[:, :, 64:128], func=AF.Copy,
                                  scale=MLA_SCALE), reads=[PS[3 + c]], writes=[q_bf])
                    yield
                qst = qTst[d.sel]
                self.transpose_to(P, ident, q_bf, [q_bf.ap[:nt, h, :] for h in range(NH)], ptR.next(), qst,
                                  lambda: qst.ap[:, :, tt * 128:tt * 128 + nt], 96, nt, "dve")
                yield
                if d.flush:
                    wcols = tt * 128 + nt
                    P.dma("sp", d.qT_scr[:, :, d.c0:d.c0 + wcols].rearrange("h r c -> r h c"), qst.ap[:, :, :wcols], reads=[qst])
                    yield

            def chain_kv(d):
                nt, t0, tt, z, tb = d.nt, d.t0, d.tt, d.z, d.tb
                P.op("act", I("activation", out=junk.ap[:nt, 0:256], in_=z.ap[:nt, 448:704], func=AF.Square,
                              accum_out=stk.ap[:nt, 0:1]), reads=[z], writes=[junk, stk])
                self.rstd_from_ss(P, stk, stk, nt, 1, 1.0 / 256)
                yield
                ck = ckvn.next()
                P.op("dve", I("scalar_tensor_tensor", out=ck.ap[:nt, :], in0=z.ap[:nt, 448:704], scalar=stk.ap[:nt, 0:1],
                              in1=kvnw.ap[:nt, :], op0=ALU.mult, op1=ALU.mult), reads=[z, stk, kvnw], writes=[ck])
                yield
                P.dma("sp", d.ckv_o[l, t0:t0 + nt, :], ck.ap[:nt, :], reads=[ck])
                P.op("act", I("activation", out=ckvn_bf.ap[:nt, :], in_=ck.ap[:nt, :], func=AF.Copy), reads=[ck], writes=[ckvn_bf])
                yield
                cst_ = ckvTst[d.sel]
                self.transpose_to(P, ident, ckvn_bf, [ckvn_bf.ap[:nt, k * 128:(k + 1) * 128] for k in range(2)],
                                  ptR.next(), cst_, lambda: cst_.ap[:, :, tt * 128:tt * 128 + nt], 128, nt, "act")
                yield
                kr = krf.next()
                P.op("dve", I("tensor_tensor", out=t1k.ap[:nt, :], in0=z.ap[:nt, 384:416], in1=tb.ap[:nt, 0:32], op=ALU.mult),
                     reads=[z, tb], writes=[t1k])
                P.op("dve", I("tensor_tensor", out=t2k.ap[:nt, :], in0=z.ap[:nt, 416:448], in1=tb.ap[:nt, 32:64], op=ALU.mult),
                     reads=[z, tb], writes=[t2k])
                yield
                P.op("dve", I("tensor_tensor", out=kr.ap[:nt, :], in0=t1k.ap[:nt, :], in1=t2k.ap[:nt, :], op=ALU.add),
                     reads=[t1k, t2k], writes=[kr])
                yield
                P.dma("sp", d.kr_o[l, t0:t0 + nt, :], kr.ap[:nt, :], reads=[kr])
                P.op("act", I("activation", out=kr_bf.ap[:nt, :], in_=kr.ap[:nt, :], func=AF.Copy), reads=[kr], writes=[kr_bf])
                yield
                pt = ptR.next()
                ptv = pt.ap.bitcast(BF16).rearrange("p (k c) -> p k c", c=128)
                P.op("pe", I("transpose", out=ptv[:32, 0, :nt], in_=kr_bf.ap[:nt, :], identity=ident.ap[:nt, :nt]),
                     reads=[kr_bf, ident], writes=[pt])
                kst = krTst[d.sel]
                P.op("act", I("activation", out=kst.ap[0:32, tt * 128:tt * 128 + nt], in_=ptv[:32, 0, :nt], func=AF.Copy),
                     reads=[pt], writes=[kst])
                yield
                if d.flush:
                    wcols = tt * 128 + nt
                    P.dma("sp", d.ckvT_scr[:, d.c0:d.c0 + wcols].rearrange("(k p) c -> p k c", p=128),
                          cst_.ap[:, :, :wcols], reads=[cst_])
                    P.dma("sp", d.krT_scr[:, d.c0:d.c0 + wcols], kst.ap[:, :wcols], reads=[kst])
                    yield

            def chain_ret(d):
                nt, t0, tt, z, tb = d.nt, d.t0, d.tt, d.z, d.tb
                Sst, Sbf = d.Sst, d.Sbf
                for (zo, Dtab, dst) in ((704, Dq, qd_bf), (1216, Dk, kd_bf)):
                    zv = z.ap[:nt, zo:zo + 512].rearrange("p (h d) -> p h d", d=64)
                    t1v = t1r.ap[:nt, :].rearrange("p (h d) -> p h d", d=32)
                    t2v = t2r.ap[:nt, :].rearrange("p (h d) -> p h d", d=32)
                    P.op("dve", I("tensor_tensor", out=t1v, in0=zv[:, :, 0:32], in1=bc_mid(tb.ap[:nt, 0:32], NH), op=ALU.mult),
                         reads=[z, tb], writes=[t1r])
                    yield
                    P.op("dve", I("tensor_tensor", out=t2v, in0=zv[:, :, 32:64], in1=bc_mid(tb.ap[:nt, 32:64], NH), op=ALU.mult),
                         reads=[z, tb], writes=[t2r])
                    yield
                    P.op("dve", I("tensor_tensor", out=t1r.ap[:nt, :], in0=t1r.ap[:nt, :], in1=t2r.ap[:nt, :], op=ALU.add),
                         reads=[t1r, t2r], writes=[t1r])
                    yield
                    P.op("dve", I("tensor_tensor", out=dst.ap[:nt, :], in0=t1r.ap[:nt, :], in1=Dtab[:nt, :], op=ALU.mult),
                         reads=[t1r, cstf], writes=[dst])
                    yield
                P.op("act", I("activation", out=v_bf.ap[:nt, :], in_=z.ap[:nt, 1728:2240], func=AF.Copy), reads=[z], writes=[v_bf])
                pt = ptR.next()
                ptv = pt.ap.bitcast(BF16).rearrange("p (k c) -> p k c", c=128)
                fns = []
                for g in range(2):
                    fns.append(I("transpose", out=ptv[:, g, :nt], in_=qd_bf.ap[:nt, g * 128:(g + 1) * 128], identity=ident.ap[:nt, :nt]))
                    fns.append(I("transpose", out=ptv[:, 2 + g, :nt], in_=kd_bf.ap[:nt, g * 128:(g + 1) * 128], identity=ident.ap[:nt, :nt]))
                P.op("pe", fns, reads=[qd_bf, kd_bf, ident], writes=[pt])
                yield
                P.op("act", I("activation", out=qkT.ap[:, :, :nt], in_=ptv[:, 0:4, :nt], func=AF.Copy), reads=[pt], writes=[qkT])
                yield
                for g in range(2):
                    P.op("dve", I("tensor_tensor", out=qbd.ap[:, g, :, :nt], in0=bc_mid(qkT.ap[:, g, :nt], 4), in1=bmq[:, :, :nt],
                                  op=ALU.mult), reads=[qkT, cstb], writes=[qbd])
                    yield
                fns = []
                for g in range(2):
                    ov = PS[3 + g].ap[:nt, :].rearrange("p (j n) -> p j n", j=4)[:, :, :nt]
                    fns.append(I("matmul", ov, lhsT=qkT.ap[:, 2 + g, :nt], rhs=qbd.ap[:, g, :, :nt], start=True, stop=True))
                P.op("pe", fns, reads=[qkT, qbd], writes=PS[3:5])
                yield
                for g in range(2):
                    ov = PS[3 + g].ap[:nt, :].rearrange("p (j n) -> p j n", j=4)[:, :, :nt]
                    P.op("dve", I("tensor_tensor", out=innerT.ap[:nt, g * 4:(g + 1) * 4, :nt], in0=ov, in1=bc_mid(causal[:nt, :nt], 4),
                                  op=ALU.mult), reads=[PS[3 + g], cstb], writes=[innerT])
                    yield
                fns = []
                for h in range(NH):
                    g, j = h // 4, h % 4
                    fns.append(I("matmul", PS[5].ap[:nt, h * 64:(h + 1) * 64], lhsT=qkT.ap[:, g, :nt],
                                 rhs=Sbf.ap[:, g, j * 64:(j + 1) * 64], start=True, stop=False))
                    fns.append(I("matmul", PS[5].ap[:nt, h * 64:(h + 1) * 64], lhsT=innerT.ap[:nt, h, :nt],
                                 rhs=v_bf.ap[:nt, h * 64:(h + 1) * 64], start=False, stop=True))
                P.op("pe", fns, reads=[qkT, Sbf, innerT, v_bf], writes=[PS[5]])
                yield
                fns = [I("matmul", PS[3].ap[:, g * 256:(g + 1) * 256], lhsT=kd_bf.ap[:nt, g * 128:(g + 1) * 128],
                         rhs=v_bf.ap[:nt, g * 256:(g + 1) * 256], start=True, stop=True) for g in range(2)]
                P.op("pe", fns, reads=[kd_bf, v_bf], writes=[PS[3]])
                yield
                P.op("dve", I("tensor_tensor", out=stmp.ap.rearrange("p (g c) -> p g c", g=2),
                              in0=PS[3].ap.rearrange("p (g c) -> p g c", g=2), in1=bc_mid(bms, 2), op=ALU.mult),
                     reads=[PS[3], cstf], writes=[stmp])
                yield
                P.op("dve", I("tensor_tensor", out=Sst.ap.rearrange("p g c -> p (g c)"), in0=Sst.ap.rearrange("p g c -> p (g c)"),
                              in1=stmp.ap, op=ALU.add), reads=[Sst, stmp], writes=[Sst])
                yield
                for g in range(2):
                    P.op("dve", I("tensor_scalar", out=Sst.ap[:, g, :], in0=Sst.ap[:, g, :], scalar1=gl2[d.Lblk][:, g:g + 1],
                                  scalar2=None, op0=ALU.mult), reads=[Sst, cstf], writes=[Sst])
                yield
                P.op("dve", I("tensor_copy", out=Sbf.ap, in_=Sst.ap), reads=[Sst], writes=[Sbf])
                yield
                osb = osbR.next()
                d.osb = osb
                P.op("act", I("activation", out=osb.ap[:nt, :], in_=PS[5].ap[:nt, :], func=AF.Copy), reads=[PS[5]], writes=[osb])
                yield
                if d.last:
                    for h in range(NH):
                        g, j = h // 4, h % 4
                        P.dma("sp", d.ret_o[l, h], Sst.ap[32 * j:32 * j + 32, g, j * 64:(j + 1) * 64], reads=[Sst])
                    yield

            def chain_ret_b(d):
                nt, t0, tt, z, osb = d.nt, d.t0, d.tt, d.z, d.osb
                ov = osb.ap[:nt, :].rearrange("p (h d) -> p h d", d=64)
                P.op("dve", I("tensor_reduce", out=gst.ap[:nt, 0:8], in_=ov, axis=AX.X, op=ALU.add), reads=[osb], writes=[gst])
                P.op("act", I("activation", out=osq.ap[:nt, :], in_=osb.ap[:nt, :], func=AF.Square), reads=[osb], writes=[osq])
                yield
                P.op("dve", I("tensor_reduce", out=gst.ap[:nt, 8:16], in_=osq.ap[:nt, :].rearrange("p (h d) -> p h d", d=64),
                              axis=AX.X, op=ALU.add), reads=[osq], writes=[gst])
                yield
                P.op("dve", I("tensor_scalar", out=gst.ap[:nt, 16:24], in0=gst.ap[:nt, 0:8], scalar1=1.0 / 64, scalar2=None,
                              op0=ALU.mult), reads=[gst], writes=[gst])
                yield
                P.op("dve", I("tensor_tensor", out=gst.ap[:nt, 24:32], in0=gst.ap[:nt, 16:24], in1=gst.ap[:nt, 16:24], op=ALU.mult),
                     reads=[gst], writes=[gst])
                yield
                P.op("dve", I("scalar_tensor_tensor", out=gst.ap[:nt, 32:40], in0=gst.ap[:nt, 8:16], scalar=1.0 / 64,
                              in1=gst.ap[:nt, 24:32], op0=ALU.mult, op1=ALU.subtract), reads=[gst], writes=[gst])
                yield
                P.op("act", I("activation", out=gst.ap[:nt, 40:48], in_=gst.ap[:nt, 32:40], func=AF.Ln, scale=1.0, bias=EPS),
                     reads=[gst], writes=[gst])
                P.op("act", I("activation", out=gst.ap[:nt, 40:48], in_=gst.ap[:nt, 40:48], func=AF.Exp, scale=-0.5),
                     reads=[gst], writes=[gst])
                P.op("act", I("activation", out=eg.ap[:nt, :], in_=z.ap[:nt, 2240:2752], func=AF.Exp, scale=-1.0),
                     reads=[z], writes=[eg])
                P.op("act", I("activation", out=eg.ap[:nt, :], in_=eg.ap[:nt, :], func=AF.Ln, scale=1.0, bias=1.0),
                     reads=[eg], writes=[eg])
                P.op("act", I("activation", out=eg.ap[:nt, :], in_=eg.ap[:nt, :], func=AF.Exp, scale=-1.0),
                     reads=[eg], writes=[eg])
                yield
                onv = on.ap[:nt, :].rearrange("p (h d) -> p h d", d=64)
                P.op("dve", I("tensor_tensor", out=onv, in0=ov, in1=bc_last(gst.ap[:nt, 16:24], 64), op=ALU.subtract),
                     reads=[osb, gst], writes=[on])
                yield
                P.op("dve", I("tensor_tensor", out=onv, in0=onv, in1=bc_last(gst.ap[:nt, 40:48], 64), op=ALU.mult),
                     reads=[on, gst], writes=[on])
                yield
                P.op("dve", I("tensor_tensor", out=on.ap[:nt, :], in0=on.ap[:nt, :], in1=gnw.ap[:nt, :], op=ALU.mult),
                     reads=[on, gnw], writes=[on])
                yield
                P.op("dve", I("tensor_tensor", out=eg.ap[:nt, :], in0=eg.ap[:nt, :], in1=z.ap[:nt, 2240:2752], op=ALU.mult),
                     reads=[eg, z], writes=[eg])
                yield
                P.op("dve", I("tensor_tensor", out=mixr.ap[:nt, :], in0=on.ap[:nt, :], in1=eg.ap[:nt, :], op=ALU.mult),
                     reads=[on, eg], writes=[mixr])
                yield
                mst = mixTst[d.sel]
                self.transpose_to(P, ident, mixr, [mixr.ap[:nt, k * 128:(k + 1) * 128] for k in range(4)],
                                  ptR.next(), mst, lambda: mst.ap[:, :, tt * 128:tt * 128 + nt], 128, nt, "act")
                yield
                if d.flush:
                    wcols = tt * 128 + nt
                    P.dma("sp", d.mixT_scr[512:1024, d.c0:d.c0 + wcols].rearrange("(k p) c -> p k c", p=128),
                          mst.ap[:, :, :wcols], reads=[mst])
                yield

            def interleave(gens):
                gens = list(gens)
                while gens:
                    for g in list(gens):
                        try:
                            next(g)
                        except StopIteration:
                            gens.remove(g)

            n = len(descs)
            print("phase A sbuf bytes remaining:", nc.sbuf_bytes_remaining)
            for i in range(n + 2):
                gens = []
                if i >= 2:
                    gens.append(chain_ret_b(descs[i - 2]))
                if 1 <= i <= n:
                    dprev = descs[i - 1]
                    gens += [chain_ret(dprev), chain_q(dprev), chain_kv(dprev)]
                if i < n:
                    gens.append(stage1(descs[i]))
                interleave(gens)
            P.finish()
            return P.ninstr

    def phase_B(self, l):
        nc = self.nc
        S, PAST = self.S, self.PAST
        NKS = self.NK_S
        with ExitStack() as st:
            P = Prog(nc, st, f"B{l}")
            sb = lambda shape, dt, name=None: self.sb(st, shape, dt, name)
            PS = self.psum(st)
            w_k = sb([128, 2, 512], BF16, "w_k"); w_v = sb([128, 2, 512], BF16, "w_v")
            self.load_w(P, w_k, self.w_k[l]); self.load_w(P, w_v, self.w_v[l])
            identf = sb([128, 128], F32); ident = sb([128, 128], BF16, "identb")
            P.dma("sp", identf.ap, self.cst[:, 0:128], writes=[identf])
            P.op("dve", I("tensor_copy", out=ident.ap, in_=identf.ap), reads=[identf], writes=[ident])
            onesf = sb([128, 64], F32, "onesf")
            P.op("dve", I("memset", onesf.ap, 1.0), writes=[onesf])
            NKT = S // 128
            NKTS = (NKS + 127) // 128
            Vp = [sb([128, NKT, 65], BF16, "Vp") for _ in range(2)]
            Vs = sb([128, NKTS, 65], BF16, "Vs")
            for v in Vp + [Vs]:
                P.op("pool", I("memset", v.ap[:, :, 64:65], 1.0), writes=[v])
            qtR = Ring([sb([96, 512], BF16, "qt") for _ in range(3)])
            PTr = Ring([sb([128, 512], BF16, "PT") for _ in range(6)])
            rsb = sb([128, 512], F32, "rsb"); rbs = sb([64, 512], F32, "rbs")
            attR = Ring([sb([64, 512], BF16, "att") for _ in range(2)])
            pSr = Ring(PS[0:4]); pOr = Ring(PS[4:6])
            pM = PS[6]; pM2 = PS[7]
            KT = sb([96, 2, S], BF16, "KT"); ckvT = sb([128, 2, S], BF16, "ckvT")
            KTs = sb([96, 1, NKS], BF16, "KTs"); ckvTs = sb([128, 2, NKS], BF16, "ckvTs")
            P.dma("sp", ckvT.ap, self.ckvT_scr.rearrange("(k p) c -> p k c", p=128), writes=[ckvT])
            for cp in range(2):
                P.dma("sp", KT.ap[64:96, cp, :], self.krT_scr, writes=[KT])
            P.dma("sp", ckvTs.ap[:, :, PAST:NKS], self.ckvTs_scr.rearrange("(k p) c -> p k c", p=128), writes=[ckvTs])
            P.dma("sp", KTs.ap[64:96, 0, PAST:NKS], self.krTs_scr, writes=[KTs])

            cbuf = sb([128, PAST // 128, 256], BF16, "cbuf"); kbuf = sb([128, PAST // 128, 96], BF16, "kbuf")
            P.dma("pool", cbuf.ap, self.cckv[l].rearrange("(t p) f -> p t f", p=128), writes=[cbuf])
            P.op("dve", I("memset", kbuf.ap[:, :, 0:64], 0.0), writes=[kbuf])
            P.dma("pool", kbuf.ap[:, :, 64:96], self.ckr[l].rearrange("(t p) f -> p t f", p=128), writes=[kbuf])
            pmR = Ring([pM, pM2])
            for t4 in range(0, PAST // 128, 4):
                nt4 = min(4, PAST // 128 - t4)
                for kc in range(2):
                    pt = pmR.next()
                    ptv = pt.ap.bitcast(BF16).rearrange("p (k c) -> p k c", c=128)
                    P.op("pe", [I("transpose", out=ptv[:, i, :], in_=cbuf.ap[:, t4 + i, kc * 128:(kc + 1) * 128],
                                                                        identity=ident.ap) for i in range(nt4)],
                         reads=[cbuf, ident], writes=[pt])
                    P.op("dve", I("tensor_copy",
                        out=ckvTs.ap[:, kc, t4 * 128:(t4 + nt4) * 128].rearrange("p (k c) -> p k c", c=128),
                        in_=ptv[:, 0:nt4, :]), reads=[pt], writes=[ckvTs])
                pt = pmR.next()
                ptv = pt.ap.bitcast(BF16).rearrange("p (k c) -> p k c", c=128)
                P.op("pe", [I("transpose", out=ptv[:96, i, :], in_=kbuf.ap[:, t4 + i, :], identity=ident.ap)
                            for i in range(nt4)], reads=[kbuf, ident], writes=[pt])
                P.op("dve", I("tensor_copy",
                    out=KTs.ap[64:96, 0, t4 * 128:(t4 + nt4) * 128].rearrange("p (k c) -> p k c", c=128),
                    in_=ptv[64:96, 0:nt4, :]), reads=[pt], writes=[KTs])

            def build_KV(h, KTt, cp, ckvTt, Vt, nkeys):
                for c0 in range(0, nkeys, 512):
                    w = min(512, nkeys - c0)
                    pm = pmR.next()
                    P.op("pe", [I("matmul", pm.ap[0:64, :w], lhsT=w_k.ap[:, kc, h * 64:(h + 1) * 64],
                                                               rhs=ckvTt.ap[:, kc, c0:c0 + w], start=(kc == 0), stop=(kc == 1))
                                for kc in range(2)], reads=[w_k, ckvTt], writes=[pm])
                    P.op("dve", I("tensor_copy", out=KTt.ap[0:64, cp, c0:c0 + w], in_=pm.ap[0:64, :w]),
                         reads=[pm], writes=[KTt])
                nkt = (nkeys + 127) // 128
                for k8 in range(0, nkt, 8):
                    n8 = min(8, nkt - k8)
                    pm = pmR.next()
                    fns = []
                    for i in range(n8):
                        k0 = (k8 + i) * 128
                        nk = min(128, nkeys - k0)
                        for kc in range(2):
                            fns.append(I("matmul",
                                pm.ap[:nk, i * 64:(i + 1) * 64], lhsT=ckvTt.ap[:, kc, k0:k0 + nk],
                                rhs=w_v.ap[:, kc, h * 64:(h + 1) * 64], start=(kc == 0), stop=(kc == 1)))
                    P.op("pe", fns, reads=[ckvTt, w_v], writes=[pm])
                    nkl = min(128, nkeys - (k8 + n8 - 1) * 128)
                    if nkl == 128:
                        P.op("act", I("activation",
                            out=Vt.ap[:, k8:k8 + n8, 0:64], in_=pm.ap[:, 0:n8 * 64].rearrange("p (k d) -> p k d", d=64),
                            func=AF.Copy), reads=[pm], writes=[Vt])
                    else:
                        if n8 > 1:
                            P.op("act", I("activation",
                                out=Vt.ap[:, k8:k8 + n8 - 1, 0:64],
                                in_=pm.ap[:, 0:(n8 - 1) * 64].rearrange("p (k d) -> p k d", d=64), func=AF.Copy),
                                reads=[pm], writes=[Vt])
                        P.op("act", I("activation",
                            out=Vt.ap[:nkl, k8 + n8 - 1, 0:64], in_=pm.ap[:nkl, (n8 - 1) * 64:n8 * 64], func=AF.Copy),
                            reads=[pm], writes=[Vt])

            def load_q(qsrc, W):
                qt = qtR.next()
                P.dma("sp", qt.ap[:, :W], qsrc, writes=[qt])
                return qt

            def attend(h, qt, W, KTt, cp, Vt, ktiles, dst):
                pO = pOr.next()
                n = len(ktiles)
                pend = []

                def issue_S(i):
                    k0, nk, c0, diag = ktiles[i]
                    pS = pSr.next()
                    P.op("pe", I("matmul", pS.ap[:nk, c0:W], lhsT=KTt.ap[0:96, cp, k0:k0 + nk],
                                                         rhs=qt.ap[0:96, c0:W], start=True, stop=True),
                         reads=[KTt, qt], writes=[pS])
                    PTb = PTr.next()
                    P.op("act", I("activation", out=PTb.ap[:nk, c0:W], in_=pS.ap[:nk, c0:W], func=AF.Exp),
                         reads=[pS], writes=[PTb])
                    if diag and nk > 64:
                        P.op("pool", I("memset", PTb.ap[64:nk, c0:c0 + 64], 0.0), writes=[PTb])
                    pend.append((i, PTb))

                def issue_PV():
                    i, PTb = pend.pop(0)
                    k0, nk, c0, diag = ktiles[i]
                    P.op("pe", I("matmul", pO.ap[0:65, c0:W], lhsT=Vt.ap[:nk, k0 // 128, 0:65],
                                                          rhs=PTb.ap[:nk, c0:W], start=(i == 0), stop=(i == n - 1),
                                                          skip_group_check=True),
                         reads=[Vt, PTb], writes=[pO])

                LOOK = 3
                for i in range(n):
                    issue_S(i)
                    if i == min(LOOK, n - 1):
                        flush_pending()
                    if i >= LOOK:
                        issue_PV()
                while pend:
                    issue_PV()
                P.op("dve", I("reciprocal", out=rsb.ap[64:65, :W], in_=pO.ap[64:65, :W]), reads=[pO], writes=[rsb])
                def fin():
                    pm = pmR.next()
                    P.op("pe", I("matmul", pm.ap[0:64, :W], lhsT=onesf.ap[64:65, 0:64], rhs=rsb.ap[64:65, :W],
                                 start=True, stop=True), reads=[onesf, rsb], writes=[pm])
                    P.op("act", I("activation", out=rbs.ap[:, :W], in_=pm.ap[0:64, :W], func=AF.Copy),
                         reads=[pm], writes=[rbs])
                    at = attR.next()
                    P.op("dve", I("tensor_tensor", out=at.ap[:, :W], in0=pO.ap[0:64, :W], in1=rbs.ap[:, :W], op=ALU.mult),
                         reads=[pO, rbs], writes=[at])
                    P.dma("sp", dst, at.ap[:, :W], reads=[at])
                pending.append(fin)

            pending = []

            def flush_pending():
                while pending:
                    pending.pop(0)()

            work = []
            for h in range(NH):
                cp = h % 2
                for j in range(S // 512):
                    ktiles = []
                    for kt in range(4 * j + 4):
                        c0 = max(0, kt - 4 * j) * 128
                        ktiles.append((kt * 128, 128, c0, kt >= 4 * j))
                    work.append(dict(h=h, first=(j == 0), samp=False, qsrc=self.qT_scr[h, :, j * 512:(j + 1) * 512], W=512,
                                     KT=KT, cp=cp, V=Vp[cp], kt=ktiles,
                                     dst=self.mixT_scr[h * 64:(h + 1) * 64, j * 512:(j + 1) * 512]))
                ktiles = [(k0, min(128, NKS - k0), 0, False) for k0 in range(0, NKS, 128)]
                work.append(dict(h=h, first=True, samp=True, qsrc=self.qTs_scr[h, :, :], W=LS, KT=KTs, cp=0, V=Vs, kt=ktiles,
                                 dst=self.mixTs_scr[h * 64:(h + 1) * 64, :]))
            nxt = load_q(work[0]["qsrc"], work[0]["W"])
            for i, wk in enumerate(work):
                qt = nxt
                if i + 1 < len(work):
                    nxt = load_q(work[i + 1]["qsrc"], work[i + 1]["W"])
                if wk["first"]:
                    if wk["samp"]:
                        build_KV(wk["h"], KTs, 0, ckvTs, Vs, NKS)
                    else:
                        build_KV(wk["h"], KT, wk["cp"], ckvT, Vp[wk["cp"]], S)
                attend(wk["h"], qt, wk["W"], wk["KT"], wk["cp"], wk["V"], wk["kt"], wk["dst"])
            flush_pending()
            P.finish()
            return P.ninstr

    def phase_C1(self, l, hsrc_p, hsrc_s):
        nc = self.nc
        S = self.S
        with ExitStack() as st:
            P = Prog(nc, st, f"C{l}")
            sb = lambda shape, dt, name=None: self.sb(st, shape, dt, name)
            PS = self.psum(st)
            w_out = sb([128, 8, D], BF16, "w_out"); w_ff1 = sb([128, 8, DFF], BF16, "w_ff1"); w_ff2 = sb([128, 32, D], BF16, "w_ff2")
            self.load_w(P, w_out, self.w_out[l]); self.load_w(P, w_ff1, self.w_ff1[l]); self.load_w(P, w_ff2, self.w_ff2[l])
            nw2 = sb([128, D], F32)
            self.load_bc(P, nw2, self.nw2[l:l + 1, :])
            identf = sb([128, 128], F32); ident = sb([128, 128], BF16, "identb")
            P.dma("sp", identf.ap, self.cst[:, 0:128], writes=[identf])
            P.op("dve", I("tensor_copy", out=ident.ap, in_=identf.ap), reads=[identf], writes=[ident])
            TW = 256
            hbR = Ring([sb([128, 2, D], F32, "hb") for _ in range(2)])
            mxR = Ring([sb([128, 8, TW], BF16, "mixT") for _ in range(2)])
            n_bf = sb([128, 2, D], BF16, "n_bf"); nTR = Ring([sb([128, 8, TW], BF16, "nT") for _ in range(2)])
            gT = sb([128, 32, TW], BF16, "gT")
            rlR = Ring([sb([128, TW], F32, "rl") for _ in range(3)])
            junk = sb([128, D], BF16); st4 = sb([128, 8], F32)
            psR = Ring(PS[0:6]); ptR = Ring(PS[6:8])

            def stage1(d):
                hsrc, mixsrc, hdst, t0, ntok = d.args
                nT = nTR.next()
                d.nT = nT
                nsub = (ntok + 127) // 128
                nts = min(128, ntok)
                hb = hbR.next(); mx = mxR.next()
                d.hb = hb
                if nsub == 2:
                    P.dma("sp", hb.ap, hsrc[t0:t0 + ntok, :].rearrange("(s p) d -> p s d", p=128), writes=[hb])
                else:
                    P.dma("sp", hb.ap[:nts, 0, :], hsrc[t0:t0 + ntok, :], writes=[hb])
                P.dma("sp", mx.ap[:, :, :ntok], mixsrc[:, t0:t0 + ntok].rearrange("(k p) c -> p k c", p=128), writes=[mx])
                for s in range(nsub):
                    for c in range(2):
                        ps = psR.next()
                        P.op("pe", [I("matmul", ps.ap[:nts, :], lhsT=mx.ap[:, k, s * 128:s * 128 + nts],
                                                                 rhs=w_out.ap[:, k, c * 512:(c + 1) * 512],
                                                                 start=(k == 0), stop=(k == 7)) for k in range(8)],
                             reads=[mx, w_out], writes=[ps])
                        P.op("dve", I("tensor_tensor", out=hb.ap[:nts, s, c * 512:(c + 1) * 512],
                                                                    in0=hb.ap[:nts, s, c * 512:(c + 1) * 512],
                                                                    in1=ps.ap[:nts, :], op=ALU.add), reads=[hb, ps], writes=[hb])
                        yield
                for s in range(nsub):
                    P.op("act", I("activation", out=junk.ap[:nts, :], in_=hb.ap[:nts, s, :], func=AF.Square,
                                                           accum_out=st4.ap[:nts, s:s + 1]), reads=[hb], writes=[junk, st4])
                self.rstd_from_ss(P, st4, st4, nts, nsub, 1.0 / D)
                for s in range(nsub):
                    P.op("dve", I("scalar_tensor_tensor", out=n_bf.ap[:nts, s, :], in0=hb.ap[:nts, s, :],
                                                                    scalar=st4.ap[:nts, s:s + 1], in1=nw2.ap[:nts, :],
                                                                    op0=ALU.mult, op1=ALU.mult),
                         reads=[hb, st4, nw2], writes=[n_bf])
                    self.transpose_to(P, ident, n_bf, [n_bf.ap[:nts, s, k * 128:(k + 1) * 128] for k in range(8)],
                                      ptR.next(), nT, lambda s=s: nT.ap[:, :, s * 128:s * 128 + nts], 128, nts, "act")
                    yield
                yield

            def stage2(d):
                hsrc, mixsrc, hdst, t0, ntok = d.args
                nsub = (ntok + 127) // 128
                nts = min(128, ntok)
                hb, nT = d.hb, d.nT
                for f in range(32):
                    ps = psR.next()
                    P.op("pe", [I("matmul", ps.ap[:, :ntok], lhsT=w_ff1.ap[:, k, f * 128:(f + 1) * 128],
                                                             rhs=nT.ap[:, k, :ntok], start=(k == 0), stop=(k == 7))
                                for k in range(8)], reads=[w_ff1, nT], writes=[ps])
                    rl = rlR.next()
                    P.op("act", I("activation", out=rl.ap[:, :ntok], in_=ps.ap[:, :ntok], func=AF.Relu),
                         reads=[ps], writes=[rl])
                    eng = "dve"
                    P.op(eng, I("tensor_tensor", out=gT.ap[:, f, :ntok], in0=rl.ap[:, :ntok], in1=rl.ap[:, :ntok],
                                                              op=ALU.mult), reads=[rl], writes=[gT])
                    if f % 4 == 3:
                        yield
                for s in range(nsub):
                    for c in range(2):
                        ps = psR.next()
                        P.op("pe", [I("matmul", ps.ap[:nts, :], lhsT=gT.ap[:, f, s * 128:s * 128 + nts],
                                                                 rhs=w_ff2.ap[:, f, c * 512:(c + 1) * 512],
                                                                 start=(f == 0), stop=(f == 31)) for f in range(32)],
                             reads=[gT, w_ff2], writes=[ps])
                        P.op("dve", I("tensor_tensor", out=hb.ap[:nts, s, c * 512:(c + 1) * 512],
                                                                    in0=hb.ap[:nts, s, c * 512:(c + 1) * 512],
                                                                    in1=ps.ap[:nts, :], op=ALU.add), reads=[hb, ps], writes=[hb])
                        yield
                if nsub == 2:
                    P.dma("sp", hdst[t0:t0 + ntok, :].rearrange("(s p) d -> p s d", p=128), hb.ap, reads=[hb])
                else:
                    P.dma("sp", hdst[t0:t0 + ntok, :], hb.ap[:nts, 0, :], reads=[hb])

            class Desc:
                pass
            descs = []
            for t in range(S // TW):
                if KNT >= 0 and t >= KNT:
                    break
                d = Desc(); d.args = (hsrc_p, self.mixT_scr, self.hbuf, t * TW, TW); descs.append(d)
            d = Desc(); d.args = (hsrc_s, self.mixTs_scr, self.hsbuf, 0, LS); descs.append(d)
            print("phase C1 sbuf bytes remaining:", nc.sbuf_bytes_remaining)

            def interleave(gens):
                gens = list(gens)
                while gens:
                    for g in list(gens):
                        try:
                            next(g)
                        except StopIteration:
                            gens.remove(g)

            n = len(descs)
            for i in range(n + 1):
                gens = []
                if i >= 1:
                    gens.append(stage2(descs[i - 1]))
                if i < n:
                    gens.append(stage1(descs[i]))
                interleave(gens)
            P.finish()
            return P.ninstr

    def phase_C2(self, l, last):
        nc = self.nc
        S = self.S
        with ExitStack() as st:
            P = Prog(nc, st, f"E{l}")
            sb = lambda shape, dt, name=None: self.sb(st, shape, dt, name)
            PS = self.psum(st)
            w_gate = sb([128, 8, D], BF16, "w_gate"); w_proj = sb([128, 2, D], BF16, "w_proj")
            self.load_w(P, w_gate, self.w_gate[l]); self.load_w(P, w_proj, self.w_proj[l])
            nw3 = sb([128, D], F32); fnw = sb([128, D], F32)
            self.load_bc(P, nw3, self.nw3[l:l + 1, :]); self.load_bc(P, fnw, self.fnw[0:1, :])
            identf = sb([128, 128], F32); ident = sb([128, 128], BF16, "identb")
            P.dma("sp", identf.ap, self.cst[:, 0:128], writes=[identf])
            P.op("dve", I("tensor_copy", out=ident.ap, in_=identf.ap), reads=[identf], writes=[ident])
            hbR = Ring([sb([128, D], F32, "hb") for _ in range(5)])
            pbR = Ring([sb([128, 256], BF16, "pb") for _ in range(2)])
            yR = Ring([sb([128, D], F32, "y") for _ in range(2)])
            nbR = Ring([sb([128, D], BF16, "n_bf") for _ in range(2)])
            nTR = Ring([sb([128, 8, 128], BF16, "nT") for _ in range(2)])
            ppTR = Ring([sb([128, 2, 128], BF16, "ppT") for _ in range(2)])
            junk = sb([128, D], BF16); st1 = sb([128, 4], F32); stf = sb([128, 4], F32)
            egR = Ring([sb([128, 512], F32, "eg") for _ in range(3)])
            psR = Ring(PS[0:6]); ptR = Ring(PS[6:8])

            class Desc:
                pass

            def stage1(d):
                nt, t0 = d.nt, d.t0
                hb = hbR.next(); pb = pbR.next(); n_bf = nbR.next(); nT = nTR.next(); ppT = ppTR.next()
                d.hb, d.nT, d.ppT = hb, nT, ppT
                P.dma("sp", hb.ap[:nt, :], d.hsrc[t0:t0 + nt, :], writes=[hb])
                P.dma("pool", pb.ap[:nt, :], d.psrc[t0:t0 + nt, :], writes=[pb])
                yield
                P.op("act", I("activation", out=junk.ap[:nt, :], in_=hb.ap[:nt, :], func=AF.Square,
                              accum_out=st1.ap[:nt, 0:1]), reads=[hb], writes=[junk, st1])
                self.rstd_from_ss(P, st1, st1, nt, 1, 1.0 / D)
                yield
                P.op("dve", I("scalar_tensor_tensor", out=n_bf.ap[:nt, :], in0=hb.ap[:nt, :], scalar=st1.ap[:nt, 0:1],
                              in1=nw3.ap[:nt, :], op0=ALU.mult, op1=ALU.mult), reads=[hb, st1, nw3], writes=[n_bf])
                yield
                self.transpose_to(P, ident, n_bf, [n_bf.ap[:nt, k * 128:(k + 1) * 128] for k in range(8)],
                                  ptR.next(), nT, lambda: nT.ap[:, :, :nt], 128, nt, "act")
                yield
                self.transpose_to(P, ident, pb, [pb.ap[:nt, k * 128:(k + 1) * 128] for k in range(2)],
                                  ptR.next(), ppT, lambda: ppT.ap[:, :, :nt], 128, nt, "act")
                yield

            def stage2(d):
                nt, t0, hb, nT, ppT = d.nt, d.t0, d.hb, d.nT, d.ppT
                for c in range(2):
                    pg = psR.next(); pq = psR.next()
                    P.op("pe", [I("matmul", pg.ap[:nt, :], lhsT=nT.ap[:, k, :nt], rhs=w_gate.ap[:, k, c * 512:(c + 1) * 512],
                                  start=(k == 0), stop=(k == 7)) for k in range(8)], reads=[nT, w_gate], writes=[pg])
                    P.op("pe", [I("matmul", pq.ap[:nt, :], lhsT=ppT.ap[:, k, :nt], rhs=w_proj.ap[:, k, c * 512:(c + 1) * 512],
                                  start=(k == 0), stop=(k == 1)) for k in range(2)], reads=[ppT, w_proj], writes=[pq])
                    yield
                    eg = egR.next()
                    P.op("act", I("activation", out=eg.ap[:nt, :], in_=pg.ap[:nt, :], func=AF.Exp, scale=-1.0),
                         reads=[pg], writes=[eg])
                    yield
                    P.op("act", I("activation", out=eg.ap[:nt, :], in_=eg.ap[:nt, :], func=AF.Ln, scale=1.0, bias=1.0),
                         reads=[eg], writes=[eg])
                    P.op("act", I("activation", out=eg.ap[:nt, :], in_=eg.ap[:nt, :], func=AF.Exp, scale=-1.0),
                         reads=[eg], writes=[eg])
                    yield
                    P.op("dve", I("tensor_tensor", out=eg.ap[:nt, :], in0=eg.ap[:nt, :], in1=pq.ap[:nt, :], op=ALU.mult),
                         reads=[eg, pq], writes=[eg])
                    yield
                    P.op("dve", I("tensor_tensor", out=hb.ap[:nt, c * 512:(c + 1) * 512], in0=hb.ap[:nt, c * 512:(c + 1) * 512],
                                  in1=eg.ap[:nt, :], op=ALU.add), reads=[hb, eg], writes=[hb])
                    yield
                yield

            def stage3(d):
                nt, t0, hb = d.nt, d.t0, d.hb
                if last:
                    P.op("act", I("activation", out=junk.ap[:nt, :], in_=hb.ap[:nt, :], func=AF.Square,
                                  accum_out=stf.ap[:nt, 0:1]), reads=[hb], writes=[junk, stf])
                    self.rstd_from_ss(P, stf, stf, nt, 1, 1.0 / D)
                    yield
                    yb = yR.next()
                    P.op("dve", I("scalar_tensor_tensor", out=yb.ap[:nt, :], in0=hb.ap[:nt, :], scalar=stf.ap[:nt, 0:1],
                                  in1=fnw.ap[:nt, :], op0=ALU.mult, op1=ALU.mult), reads=[hb, stf, fnw], writes=[yb])
                    yield
                    P.dma("sp", d.ydst[t0:t0 + nt, :], yb.ap[:nt, :], reads=[yb])
                else:
                    P.dma("sp", d.hdst[t0:t0 + nt, :], hb.ap[:nt, :], reads=[hb])
                yield

            descs = []
            for t in range(S // 128 + 1):
                if KNT >= 0 and KNT <= t < S // 128:
                    continue
                d = Desc()
                is_s = (t == S // 128)
                d.nt = LS if is_s else 128
                d.t0 = 0 if is_s else t * 128
                d.hsrc = self.hsbuf if is_s else self.hbuf
                d.hdst = d.hsrc
                d.psrc = self.pps[l] if is_s else self.pp[l]
                d.ydst = self.ys if is_s else self.y
                descs.append(d)

            def interleave(gens):
                gens = list(gens)
                while gens:
                    for g in list(gens):
                        try:
                            next(g)
                        except StopIteration:
                            gens.remove(g)

            n = len(descs)
            for i in range(n + 2):
                gens = []
                if i >= 2:
                    gens.append(stage3(descs[i - 2]))
                if 1 <= i <= n:
                    gens.append(stage2(descs[i - 1]))
                if i < n:
                    gens.append(stage1(descs[i]))
                interleave(gens)
            P.finish()
            return P.ninstr

    def build(self):
        nc = self.nc
        S, PAST = self.S, self.PAST
        tot = 0
        import os
        ph = os.environ.get("KPH", "A0B0C0E0A1B1C1E1")
        for l in range(2):
            hp = self.x if l == 0 else self.hbuf
            hs = self.xs if l == 0 else self.hsbuf
            if f"A{l}" in ph:
                tot += self.phase_A(l, hp, hs, 128)
            if f"B{l}" in ph:
                tot += self.phase_B(l)
            if f"C{l}" in ph:
                tot += self.phase_C1(l, hp, hs)
            if f"E{l}" in ph:
                tot += self.phase_C2(l, l == 1 or f"E{l+1}" not in ph)
        self.ninstr = tot
        return nc


_CACHE = {}


def _swap16(a):
    return np.concatenate([a[..., 16:], a[..., :16]], axis=-1)


def _consts(S, PAST):
    half = 16
    inv = (10000.0 ** (-np.arange(half, dtype=np.float32) / half)).astype(np.float32)

    def tabs(pos):
        ang = pos.astype(np.float32)[:, None] * inv[None, :]
        c, s = np.cos(ang).astype(np.float32), np.sin(ang).astype(np.float32)
        cs = np.concatenate([c, c], -1)
        sn = np.concatenate([-s, s], -1)
        return np.concatenate([cs, sn, cs * np.float32(MLA_SCALE), sn * np.float32(MLA_SCALE)], -1).astype(np.float32)

    tab_p = tabs(np.arange(S))
    tab_s = tabs(PAST + np.arange(LS))
    lg = np.log1p(-np.exp2(-5.0 - np.arange(NH, dtype=np.float64)))
    n = np.arange(128, dtype=np.float64)
    dq = np.exp((n[:, None] + 1.0) * lg[None, :])
    dk = np.exp(-(n[:, None] + 1.0) * lg[None, :]) * (32 ** -0.5)
    cst = np.zeros((128, 2176), np.float32)
    cst[:, 0:128] = np.eye(128)
    cst[:, 128:256] = (np.arange(128)[None, :] >= np.arange(128)[:, None])
    p = np.arange(128)
    bm = (p[:, None] // 32 == np.arange(4)[None, :]).astype(np.float32)
    cst[:, 256:768] = np.repeat(bm, 128, axis=1)
    cst[:, 768:1024] = np.repeat(bm, 64, axis=1)
    cst[:, 1024:1280] = np.repeat(dq, 32, axis=1)
    cst[:, 1280:1536] = np.repeat(dk, 32, axis=1)
    for (L, c0) in ((128, 2048), (LS, 2050)):
        for g in range(2):
            cst[:, c0 + g] = np.exp(L * lg[4 * g + p // 32])
    return tab_p, tab_s, cst


def _prep_weights(w_in, w_uq, w_ukv):
    q_lat = w_in[:, :, 0:384]; ckv = w_in[:, :, 384:640]; kr = w_in[:, :, 640:672]
    rq = w_in[:, :, 672:928].reshape(2, D, NH, 32); rk = w_in[:, :, 928:1184].reshape(2, D, NH, 32)
    rv = w_in[:, :, 1184:1696]; rg = w_in[:, :, 1696:2208]
    rq2 = np.concatenate([rq, _swap16(rq)], -1).reshape(2, D, 512)
    rk2 = np.concatenate([rk, _swap16(rk)], -1).reshape(2, D, 512)
    w_in_p = np.concatenate([q_lat, kr, _swap16(kr), ckv, rq2, rk2, rv, rg], -1)
    assert w_in_p.shape[-1] == NZ
    uq = w_uq.reshape(2, 384, NH, 96)
    w_uq_p = np.concatenate([uq[..., 64:96], _swap16(uq[..., 64:96]), uq[..., 0:64]], -1).reshape(2, 384, 1024)
    ukv = w_ukv.reshape(2, 256, NH, 128)
    w_k = ukv[..., 0:64].reshape(2, 256, 512)
    w_v = ukv[..., 64:128].reshape(2, 256, 512)
    c = np.ascontiguousarray
    return c(w_in_p), c(w_uq_p), c(w_k), c(w_v)


def kernel(x_prompt, x_sample, cache_ckv, cache_krope, state_ret, p_prompt, p_sample,
           norm_mix_w, w_in, q_norm_w, w_uq, kv_norm_w, w_ukv, ret_gn_w, w_out,
           norm_ffn_w, w_ff1, w_ff2, norm_ple_w, w_ple_gate, w_ple_proj, final_norm_w):
    f = lambda a: np.ascontiguousarray(np.asarray(a, dtype=np.float32))
    x_prompt, x_sample, cache_ckv, cache_krope, state_ret, p_prompt, p_sample = map(
        f, (x_prompt, x_sample, cache_ckv, cache_krope, state_ret, p_prompt, p_sample))
    B, S, _ = x_prompt.shape
    Bd = x_sample.shape[0]
    PAST = cache_ckv.shape[2]
    key = (S, PAST)
    if key not in _CACHE:
        _CACHE[key] = Builder(S, PAST).build()
    nc = _CACHE[key]
    tab_p, tab_s, cst = _consts(S, PAST)
    w_in_p, w_uq_p, w_k, w_v = _prep_weights(f(w_in), f(w_uq), f(w_ukv))
    shared = dict(w_in=w_in_p, w_uq=w_uq_p, w_k=w_k, w_v=w_v, w_out=f(w_out), w_ff1=f(w_ff1), w_ff2=f(w_ff2),
                  w_gate=f(w_ple_gate), w_proj=f(w_ple_proj), nw1=f(norm_mix_w), qnw=f(q_norm_w), kvnw=f(kv_norm_w),
                  gnw=f(ret_gn_w), nw2=f(norm_ffn_w), nw3=f(norm_ple_w), fnw=f(final_norm_w).reshape(1, D),
                  tab_p=tab_p, tab_s=tab_s, cst=cst)
    ncores = 8
    in_maps = []
    for c in range(ncores):
        b = (c // 2) % B
        sidx = c % Bd
        m = dict(shared)
        m.update(x=x_prompt[b], xs=x_sample[sidx], cckv=f(cache_ckv[:, sidx]), ckr=f(cache_krope[:, sidx]),
                 sret=f(state_ret[:, sidx]), pp=f(p_prompt[:, b]), pps=f(p_sample[:, sidx]))
        in_maps.append(m)
    res = run_bass_kernel_spmd(nc, in_maps, core_ids=list(range(ncores)))
    r = res.results
    global LAST_RES
    LAST_RES = r
    pc = [2 * b for b in range(B)]
    y_prompt = np.stack([r[c]["y"] for c in pc])
    ckv_prompt = np.stack([r[c]["ckv_o"] for c in pc], axis=1)
    krope_prompt = np.stack([r[c]["kr_o"] for c in pc], axis=1)
    ret_prompt = np.stack([r[c]["ret_o"] for c in pc], axis=1)
    y_sample = np.stack([r[c]["ys"] for c in range(Bd)])
    ckv_sample = np.stack([r[c]["ckvs_o"] for c in range(Bd)], axis=1)
    krope_sample = np.stack([r[c]["krs_o"] for c in range(Bd)], axis=1)
    ret_sample = np.stack([r[c]["rets_o"] for c in range(Bd)], axis=1)
    out = (y_prompt, y_sample, ckv_prompt, krope_prompt, ret_prompt, ckv_sample, krope_sample, ret_sample)
    return tuple(np.ascontiguousarray(o, dtype=np.float32) for o in out)
```
